# Optimizing a Trainium2 kernel written in Bass

```python
import jax
import jax.numpy as jnp
from jax import lax
import numpy as np

D_MODEL = 2048
BATCH = 4
SEQ = 8192
DEPTH = 1
DEC_BATCH = 1
DEC_SEQ = 16384
PAST_LEN = 128

GRID_W = 64
HEAD_DIM = 64
ATT_WIDTH = D_MODEL // 2
RWKV_WIDTH = D_MODEL - ATT_WIDTH
N_ATT_HEADS = ATT_WIDTH // HEAD_DIM
N_RWKV_HEADS = RWKV_WIDTH // HEAD_DIM
WIN_ROWS_MAX = 8
WIN_COLS = 16
DECAY_LORA = 64
ICLR_LORA = 64
GATE_LORA = 160
N_DIR = 2
ATT_COLS = 3 * ATT_WIDTH
RWKV_COLS = 3 * RWKV_WIDTH + N_DIR * DECAY_LORA + N_DIR * ICLR_LORA + GATE_LORA
PROJ_COLS = ATT_COLS + RWKV_COLS
D_FF = ((8 * D_MODEL + 3 * 256 - 1) // (3 * 256)) * 256
RMS_EPS = 1e-6
LNX_EPS = 64e-5

kernel_name = 'hymba_natten_rwkv7_bidir_encoder'


def rmsnorm(x, g):
    xf = x.astype(jnp.float32)
    y = xf * lax.rsqrt(jnp.mean(xf * xf, axis=-1, keepdims=True) + RMS_EPS)
    return (y * g.astype(jnp.float32)).astype(x.dtype)


def neighbourhood_attention(q, k, v, rpb):
    b, t, h, d = q.shape
    rows = t // GRID_W
    kr = min(WIN_ROWS_MAX, rows)
    qg = q.reshape(b, rows, GRID_W, h, d)
    kg = k.reshape(b, rows, GRID_W, h, d)
    vg = v.reshape(b, rows, GRID_W, h, d)
    col = jnp.arange(GRID_W)
    col_start = jnp.clip(col - WIN_COLS // 2, 0, GRID_W - WIN_COLS)
    col_idx = col_start[:, None] + jnp.arange(WIN_COLS)[None, :]
    dcol = col_idx - col[:, None] + (WIN_COLS - 1)
    rpb = rpb.astype(jnp.float32)
    scale = d ** -0.5

    def row_block(r):
        r0 = jnp.clip(r - kr // 2, 0, rows - kr)
        q_r = lax.dynamic_index_in_dim(qg, r, axis=1, keepdims=False)
        k_r = lax.dynamic_slice_in_dim(kg, r0, kr, axis=1)
        v_r = lax.dynamic_slice_in_dim(vg, r0, kr, axis=1)
        k_n = k_r[:, :, col_idx]
        v_n = v_r[:, :, col_idx]
        drow = r0 + jnp.arange(kr) - r + (WIN_ROWS_MAX - 1)
        bias = rpb[:, drow[None, :, None], dcol[:, None, :]]
        s = jnp.einsum('bqhd,bkqjhd->bhqkj', q_r, k_n, preferred_element_type=jnp.float32)
        s = s * scale + bias
        p = jax.nn.softmax(s.reshape(b, h, GRID_W, kr * WIN_COLS), axis=-1)
        p = p.reshape(b, h, GRID_W, kr, WIN_COLS).astype(v.dtype)
        return jnp.einsum('bhqkj,bkqjhd->bqhd', p, v_n)

    out = lax.map(row_block, jnp.arange(rows))
    return jnp.transpose(out, (1, 0, 2, 3, 4)).reshape(b, t, h, d)


def rwkv7_time_mix(u, mu_prev, mu_next, w0, w2, a0, a2, g2, k_k, k_a, r_k, lnx_w, lnx_b):
    b, t, _ = u.shape
    c, h, n = RWKV_WIDTH, N_RWKV_HEADS, HEAD_DIM
    out_dtype = u.dtype
    u_prev = jnp.pad(u[:, :-1], ((0, 0), (1, 0), (0, 0)))
    u_next = jnp.pad(u[:, 1:], ((0, 0), (0, 1), (0, 0)))
    u = u + mu_prev * (u_prev - u) + mu_next * (u_next - u)
    splits = [c, 2 * c, 3 * c, 3 * c + N_DIR * DECAY_LORA, 3 * c + N_DIR * DECAY_LORA + N_DIR * ICLR_LORA]
    r, k, v, wd, ad, gd = jnp.split(u.astype(jnp.float32), splits, axis=-1)
    wd = wd.reshape(b, t, N_DIR, DECAY_LORA)
    ad = ad.reshape(b, t, N_DIR, ICLR_LORA)
    w_log = -jax.nn.softplus(-(w0 + jnp.einsum('btel,elc->btec', jnp.tanh(wd), w2))) - 0.5
    decay = jnp.exp(-jnp.exp(w_log))
    a = jax.nn.sigmoid(a0 + jnp.einsum('btel,elc->btec', ad, a2))
    g = jnp.einsum('btl,lc->btc', jax.nn.sigmoid(gd), g2)
    kk = (k * k_k).reshape(b, t, h, n)
    kk = (kk / jnp.maximum(jnp.sqrt(jnp.sum(kk * kk, axis=-1, keepdims=True)), 1e-12)).reshape(b, t, c)
    k_dir = k[:, :, None, :] * (1.0 + (a - 1.0) * k_a)

    def dirs(z):
        z = z.reshape(b, t, N_DIR, h, n)
        z = jnp.stack([z[:, :, 0], jnp.flip(z[:, :, 1], axis=1)], axis=0)
        return jnp.transpose(z, (2, 0, 1, 3, 4))

    def both(z):
        return jnp.broadcast_to(z[:, :, None, :], (b, t, N_DIR, c))

    xs = (dirs(both(r)), dirs(decay), dirs(k_dir), dirs(both(v)),
          dirs(both(-kk)), dirs(kk[:, :, None, :] * a))

    def step(S, inp):
        r_t, w_t, k_t, v_t, z_t, bb_t = inp
        sz = jnp.einsum('ebhvk,ebhk->ebhv', S, z_t)
        S = S * w_t[..., None, :] + sz[..., None] * bb_t[..., None, :] + v_t[..., None] * k_t[..., None, :]
        return S, jnp.einsum('ebhvk,ebhk->ebhv', S, r_t)

    S0 = jnp.zeros((N_DIR, b, h, n, n), jnp.float32)
    _, ys = lax.scan(step, S0, xs)
    y = ys[:, 0] + jnp.flip(ys[:, 1], axis=0)
    y = jnp.transpose(y, (1, 0, 2, 3))
    mu = jnp.mean(y, axis=-1, keepdims=True)
    var = jnp.mean(jnp.square(y - mu), axis=-1, keepdims=True)
    yn = ((y - mu) * lax.rsqrt(var + LNX_EPS)).reshape(b, t, c) * lnx_w + lnx_b
    bonus = jnp.sum((r * (k_dir[:, :, 0] + k_dir[:, :, 1])).reshape(b, t, h, n) * r_k,
                    axis=-1, keepdims=True) * v.reshape(b, t, h, n)
    out = (yn + bonus.reshape(b, t, c)) * g
    return out.astype(out_dtype)


def encoder_trunk(x, norm1_g, w_in, attn_rpb, attn_out_g, mu_prev, mu_next, w0, w2, a0, a2, g2,
                  k_k, k_a, r_k, lnx_w, lnx_b, w_out, norm2_g, w_gate, w_up, w_down, final_g):
    b, t, _ = x.shape
    for l in range(DEPTH):
        hn = rmsnorm(x, norm1_g[l])
        proj = jnp.einsum('btd,dp->btp', hn, w_in[l])
        q, k, v = jnp.split(proj[..., :ATT_COLS], 3, axis=-1)
        q = q.reshape(b, t, N_ATT_HEADS, HEAD_DIM)
        k = k.reshape(b, t, N_ATT_HEADS, HEAD_DIM)
        v = v.reshape(b, t, N_ATT_HEADS, HEAD_DIM)
        att = neighbourhood_attention(q, k, v, attn_rpb[l]).reshape(b, t, ATT_WIDTH)
        att = rmsnorm(att, attn_out_g[l])
        rw = rwkv7_time_mix(proj[..., ATT_COLS:], mu_prev[l], mu_next[l], w0[l], w2[l], a0[l], a2[l],
                            g2[l], k_k[l], k_a[l], r_k[l], lnx_w[l], lnx_b[l])
        x = x + jnp.einsum('btc,cd->btd', jnp.concatenate([att, rw], axis=-1), w_out[l])
        hn = rmsnorm(x, norm2_g[l])
        ff = jax.nn.silu(hn @ w_gate[l]) * (hn @ w_up[l])
        x = x + ff @ w_down[l]
    return rmsnorm(x, final_g)


def setup_inputs(seed: int = 0) -> dict:
    key = jax.random.key(seed)
    ks = jax.random.split(key, 26)
    f = jnp.float32
    L, C, H = DEPTH, RWKV_WIDTH, N_RWKV_HEADS

    def nrm(k, shape, s):
        return jax.random.normal(k, shape, f) * s

    return {
        'x_prompt': nrm(ks[0], (BATCH, SEQ, D_MODEL), 1.0),
        'x_sample': nrm(ks[1], (DEC_BATCH, DEC_SEQ, D_MODEL), 1.0),
        'norm1_g': 1.0 + nrm(ks[2], (L, D_MODEL), 0.05),
        'w_in': nrm(ks[3], (L, D_MODEL, PROJ_COLS), D_MODEL ** -0.5),
        'attn_rpb': nrm(ks[4], (L, N_ATT_HEADS, 2 * WIN_ROWS_MAX - 1, 2 * WIN_COLS - 1), 0.1),
        'attn_out_g': 1.0 + nrm(ks[5], (L, ATT_WIDTH), 0.05),
        'mu_prev': jax.random.uniform(ks[6], (L, RWKV_COLS), f, 0.0, 0.5),
        'mu_next': jax.random.uniform(ks[7], (L, RWKV_COLS), f, 0.0, 0.5),
        'w0': jax.random.uniform(ks[8], (L, N_DIR, C), f, -6.0, 1.0),
        'w2': nrm(ks[9], (L, N_DIR, DECAY_LORA, C), 0.1),
        'a0': nrm(ks[10], (L, N_DIR, C), 0.1),
        'a2': nrm(ks[11], (L, N_DIR, ICLR_LORA, C), 0.1),
        'g2': nrm(ks[12], (L, GATE_LORA, C), GATE_LORA ** -0.5),
        'k_k': 0.85 + nrm(ks[13], (L, C), 0.05),
        'k_a': 1.0 + nrm(ks[14], (L, C), 0.05),
        'r_k': nrm(ks[15], (L, H, HEAD_DIM), 0.1),
        'lnx_w': 1.0 + nrm(ks[16], (L, C), 0.05),
        'lnx_b': nrm(ks[17], (L, C), 0.02),
        'w_out': nrm(ks[18], (L, D_MODEL, D_MODEL), D_MODEL ** -0.5),
        'norm2_g': 1.0 + nrm(ks[19], (L, D_MODEL), 0.05),
        'w_gate': nrm(ks[20], (L, D_MODEL, D_FF), D_MODEL ** -0.5),
        'w_up': nrm(ks[21], (L, D_MODEL, D_FF), D_MODEL ** -0.5),
        'w_down': nrm(ks[22], (L, D_FF, D_MODEL), D_FF ** -0.5),
        'final_g': 1.0 + nrm(ks[23], (D_MODEL,), 0.05),
    }


def reference(x_prompt, x_sample, norm1_g, w_in, attn_rpb, attn_out_g, mu_prev, mu_next, w0, w2, a0, a2,
              g2, k_k, k_a, r_k, lnx_w, lnx_b, w_out, norm2_g, w_gate, w_up, w_down, final_g):
    y_prompt = encoder_trunk(x_prompt, norm1_g, w_in, attn_rpb, attn_out_g, mu_prev, mu_next, w0, w2, a0, a2,
                             g2, k_k, k_a, r_k, lnx_w, lnx_b, w_out, norm2_g, w_gate, w_up, w_down, final_g)
    y_sample = encoder_trunk(x_sample, norm1_g, w_in, attn_rpb, attn_out_g, mu_prev, mu_next, w0, w2, a0, a2,
                             g2, k_k, k_a, r_k, lnx_w, lnx_b, w_out, norm2_g, w_gate, w_up, w_down, final_g)
    return (y_prompt, y_sample)
```

```python
import numpy as np
import concourse.bass as bass
import concourse.mybir as mybir
from concourse.bass_utils import run_bass_kernel_spmd

F32 = mybir.dt.float32
BF16 = mybir.dt.bfloat16
AF = mybir.ActivationFunctionType
ALU = mybir.AluOpType

D = 2048
KC = 16
PCOLS = 6560
DFF = 5632
FC = 44
NCORES = 8
C0 = float(np.exp(-0.5))


class Sched:
    ENG = ("pe", "act", "dve", "pool", "sp")

    def __init__(self, esems, dsems):
        self.q = {e: [] for e in self.ENG}
        self.cnt = {e: 0 for e in self.ENG}
        self.seen = {e: {} for e in self.ENG}
        self.buf = {}
        self.esems = esems
        self.dsems = dsems
        self.dused = [0] * len(dsems)
        self.drr = 0

    def _need(self, e, key, v, waits):
        if key[0] == "e" and key[1] == e and e == "pe":
            return
        if self.seen[e].get(key, 0) >= v:
            return
        self.seen[e][key] = v
        waits.append((key, v))

    def op(self, e, fn, r=(), w=(), dma=False):
        waits = []
        for k in r:
            b = self.buf.get(k)
            if b:
                for key, v in b[0].items():
                    self._need(e, key, v, waits)
        for k in w:
            b = self.buf.get(k)
            if b:
                for key, v in b[0].items():
                    self._need(e, key, v, waits)
                for key, v in b[1].items():
                    self._need(e, key, v, waits)
        if dma:
            j = self.drr
            self.drr = (self.drr + 1) % len(self.dsems)
            if self.dused[j] > 0:
                self._need(e, ("d", j), 16 * self.dused[j], waits)
            self.dused[j] += 1
            key, v = ("d", j), 16 * self.dused[j]
        else:
            self.cnt[e] += 1
            key, v = ("e", e), self.cnt[e]
        for k in r:
            b = self.buf.setdefault(k, [{}, {}])
            b[1][key] = max(b[1].get(key, 0), v)
        for k in w:
            b = self.buf.setdefault(k, [{}, {}])
            b[1].clear()
            b[0][key] = max(b[0].get(key, 0), v)
        self.q[e].append((waits, fn, key, v))

    def sem(self, key):
        return self.esems[key[1]] if key[0] == "e" else self.dsems[key[1]]

    def replay(self, e, eng):
        for waits, fn, key, v in self.q[e]:
            for wk, wv in waits:
                eng.wait_ge(self.sem(wk), wv)
            if fn is None:
                continue
            ins = fn(eng)
            ins.then_inc(self.sem(key), 16 if key[0] == "d" else 1)

    def barrier(self):
        snap = [(("e", p), self.cnt[p]) for p in self.ENG if self.cnt[p]]
        snap += [(("d", j), 16 * u) for j, u in enumerate(self.dused) if u]
        for e in self.ENG:
            waits = []
            for key, v in snap:
                if key[0] == "e" and key[1] == e and e == "pe":
                    pass
                if self.seen[e].get(key, 0) >= v:
                    continue
                self.seen[e][key] = v
                waits.append((key, v))
            self.q[e].append((waits, None, None, None))
        self.buf.clear()

    def finish(self, e="sp"):
        waits = []
        for j, u in enumerate(self.dused):
            if u:
                self._need(e, ("d", j), 16 * u, waits)
        return waits


def build(RS, dbg=(), stop_after="D"):
    T = 2 * RS * 64
    NT = T // 512
    NB = T // 128
    HT = T // 2
    NR = 2 * RS
    nc = bass.Bass("TRN2", target_bir_lowering=False)
    dbg = set(dbg)
    PH = "0A2BCD"
    last = PH.index(stop_after)

    def din(name, shape, dt=F32):
        return nc.dram_tensor(name, list(shape), dt, kind="ExternalInput")

    def dsc(name, shape, dt):
        return nc.dram_tensor(name, list(shape), dt, kind=("ExternalOutput" if name in dbg else "Internal"))

    x_in = din("x", [T, D])
    w_in = din("w_in", [D, PCOLS])
    w_out = din("w_out", [D, D])
    w_gate = din("w_gate", [D, DFF])
    w_up = din("w_up", [D, DFF])
    w_down = din("w_down", [DFF, D])
    pk16 = din("pk16", [128, 3, 16])
    pk8 = din("pk8", [128, 10, 8])
    mu_in = din("mu", [128, 2, 28])
    w2_in = din("w2", [128, 1024])
    a2_in = din("a2", [128, 1024])
    g2_in = din("g2", [160, 1024])
    link_in = din("link", [128, 1])
    cst_in = din("cst", [128, 1536])
    tab_in = din("tab", [8, 128, 16, 4, 64])
    wtab_in = din("wtab", [7, 128, 16, 8, 64])
    y_out = nc.dram_tensor("y", [T, D], F32, kind="ExternalOutput")

    wbf_in = dsc("wbf_in", [D, PCOLS], BF16)
    wbf_out = dsc("wbf_out", [D, D], BF16)
    wbf_g = dsc("wbf_g", [D, DFF], BF16)
    wbf_u = dsc("wbf_u", [D, DFF], BF16)
    wbf_d = dsc("wbf_d", [DFF, D], BF16)
    xT_s = dsc("xT_s", [KC, 128, T], F32)
    qT_s = dsc("qT_s", [1024, T], BF16)
    kT_s = dsc("kT_s", [1024, T], BF16)
    v_s = dsc("v_s", [T, 1024], BF16)
    raw_s = dsc("raw_s", [28 * 128, T + 2], F32)
    fm_s = [dsc(f"fm_s{e}", [8, 128, NB, 4, 128], BF16) for e in range(2)]
    tm_s = [dsc(f"tm_s{e}", [NB, 128, 2, 1024], BF16) for e in range(2)]
    vtm_s = dsc("vtm_s", [NB, 128, 1024], BF16)
    gC_s = [dsc(f"gC_s{e}", [128, 8, NB], F32) for e in range(2)]
    g_s = dsc("g_s", [1024, T], F32)
    bt_s = dsc("bt_s", [1024, T], F32)
    y_s = [dsc(f"y_s{e}", [1024, T], F32) for e in range(2)]
    attT_s = dsc("attT_s", [1024, T], F32)

    import contextlib
    es = contextlib.ExitStack()

    def sb(name, shape, dt):
        return es.enter_context(nc.sbuf_tensor(name, list(shape), dt))

    with es:
        esems = {e: es.enter_context(nc.semaphore("s_" + e)) for e in Sched.ENG}
        dsems = [es.enter_context(nc.semaphore(f"d{j}")) for j in range(48)]
        S = Sched(esems, dsems)
        ps = [es.enter_context(nc.psum_tensor(f"ps{i}", [128, 512], F32)) for i in range(8)]
        PK = [f"ps{i}" for i in range(8)]
        ARN = 48000
        arena = sb("arena", [128, ARN], F32)
        apos = [0]

        def areset():
            apos[0] = 0

        def take(shape, dt=F32):
            n = int(np.prod(shape[1:]))
            cols = n if dt == F32 else (n + 1) // 2
            cols = (cols + 7) // 8 * 8
            a = apos[0]
            apos[0] += cols
            assert apos[0] <= ARN, ("arena overflow", apos[0])
            v = arena[: shape[0], a:a + cols]
            if dt != F32:
                v = v.bitcast(dt)
            v = v[:, 0:n]
            if len(shape) == 3:
                v = v.rearrange("p (a b) -> p a b", a=shape[1])
            elif len(shape) == 4:
                v = v.rearrange("p (a b c) -> p a b c", a=shape[1], b=shape[2])
            return v

        def bc(ap2, reps):
            a = ap2.ap
            assert len(a) == 2
            return bass.AP(ap2.tensor, ap2.offset, [list(a[0]), [0, reps], list(a[1])])

        def bcl(ap2, reps):
            a = ap2.ap
            assert len(a) == 2
            return bass.AP(ap2.tensor, ap2.offset, [list(a[0]), list(a[1]), [0, reps]])

        def mm(out, lhsT, rhs, start, stop, r, w):
            S.op("pe", lambda e: e.matmul(out, lhsT=lhsT, rhs=rhs, start=start, stop=stop), r, w)

        def tr(out, in_, ident, r, w):
            S.op("pe", lambda e: e.transpose(out, in_, ident), r, w)

        def act(out, in_, func, r, w, bias=0.0, scale=1.0):
            S.op("act", lambda e: e.activation(out, in_, func, bias=bias, scale=scale), r, w)

        def tt(eng, out, in0, in1, op, r, w):
            S.op(eng, lambda e: e.tensor_tensor(out, in0, in1, op), r, w)

        def ts(eng, out, in0, s1, op0, r, w, s2=None, op1=None):
            if op1 is None:
                S.op(eng, lambda e: e.tensor_scalar(out, in0, s1, None, op0), r, w)
            else:
                S.op(eng, lambda e: e.tensor_scalar(out, in0, s1, s2, op0, op1), r, w)

        def stt(out, in0, scalar, in1, op0, op1, r, w):
            S.op("dve", lambda e: e.scalar_tensor_tensor(out, in0, scalar, in1, op0, op1), r, w)

        def cp(eng, out, in_, r, w):
            if eng == "act":
                S.op("act", lambda e: e.copy(out, in_), r, w)
            else:
                S.op(eng, lambda e: e.tensor_copy(out, in_), r, w)

        def recip(out, in_, r, w):
            S.op("dve", lambda e: e.reciprocal(out, in_), r, w)

        def dma(out, in_, r, w, q="sp", slow=False):
            if slow:
                S.op(q, lambda e: e.dma_start(out=out, in_=in_, allow_slow_non_contiguous=True), r, w, dma=True)
            else:
                S.op(q, lambda e: e.dma_start(out=out, in_=in_), r, w, dma=True)

        def dst(out, in_, r, w, slow=False):
            dma(out, in_, r, w, q="pool", slow=slow)

        cpi = [0]

        cpr = [2]

        def cpa(out, in_, r, w):
            cpi[0] = (cpi[0] + 1) % cpr[0]
            cp("dve" if cpi[0] == 0 else "act", out, in_, r, w)

        psi = [0]

        def nps():
            psi[0] = (psi[0] + 1) % 8
            return psi[0]

        C = ("consts",)
        ident_f = sb("ident_f", [128, 128], F32)
        ident_b = sb("ident_b", [128, 128], BF16)
        ones_b = sb("ones_b", [128, 128], BF16)
        blk_f = sb("blk_f", [128, 128], F32)
        msk4 = sb("msk4", [128, 2, 512], F32)
        mskl = sb("mskl", [128, 2, 128], F32)
        rst = sb("rst", [128, 512], F32)
        c16 = sb("c16", [128, 3, 16], F32)
        c8 = sb("c8", [128, 10, 8], F32)
        cmu = sb("cmu", [128, 3, 28], F32)
        omka = sb("omka", [128, 8], F32)
        linkt = sb("linkt", [128, 1], F32)
        epsc = sb("epsc", [128, 2], F32)
        w2b = sb("w2b", [128, 1024], BF16)
        a2b = sb("a2b", [128, 1024], BF16)
        g2b0 = sb("g2b0", [128, 1024], BF16)
        g2b1 = sb("g2b1", [32, 1024], BF16)
        zcol = sb("zcol", [128, 28, 2], F32)

        areset()
        cst = take([128, 1536])
        ldf = take([128, 1024])
        dma(cst, cst_in[:, :], (), ("cst",))
        cp("dve", ident_f[:], cst[:, 0:128], ("cst",), C)
        cp("dve", ident_b[:], cst[:, 0:128], ("cst",), C)
        S.op("dve", lambda e: e.memset(ones_b[:], 1.0), (), C)
        for e_ in range(2):
            cp("dve", msk4[:, e_, 0:256], cst[:, 128 + e_ * 256:128 + (e_ + 1) * 256], ("cst",), C)
            cp("dve", msk4[:, e_, 256:512], cst[:, 128 + e_ * 256:128 + (e_ + 1) * 256], ("cst",), C)
        cp("dve", mskl[:].rearrange("p a c -> p (a c)"), cst[:, 640:896], ("cst",), C)
        cp("dve", blk_f[:], cst[:, 896:1024], ("cst",), C)
        cp("dve", rst[:], cst[:, 1024:1536], ("cst",), C)
        S.op("dve", lambda e: e.memset(zcol[:], 0.0), (), C)
        S.op("dve", lambda e: e.memset(epsc[:, 0:1], 1e-6), (), C)
        S.op("dve", lambda e: e.memset(epsc[:, 1:2], 64e-5), (), C)
        dma(c16[:], pk16[:, :, :], (), C)
        dma(c8[:], pk8[:, :, :], (), C)
        dma(cmu[:, 0:2, :], mu_in[:, :, :], (), C)
        dma(linkt[:], link_in[:, :], (), C)
        tt("dve", cmu[:, 2, :], cmu[:, 0, :], cmu[:, 1, :], ALU.add, C, C)
        ts("dve", cmu[:, 2, :], cmu[:, 2, :], -1.0, ALU.mult, C, C, 1.0, ALU.add)
        ts("dve", omka[:], c8[:, 2, :], -1.0, ALU.mult, C, C, 1.0, ALU.add)
        for src, dstt in ((w2_in, w2b), (a2_in, a2b), (g2_in, g2b0)):
            dma(ldf, src[0:128, :], (), ("ldf",))
            cp("dve", dstt[:, :], ldf, ("ldf",), C)
        dma(ldf[:32, :], g2_in[128:160, :], (), ("ldf",))
        cp("dve", g2b1[:, :], ldf[:32, :], ("ldf",), C)
        dst(raw_s[:, 0:1].rearrange("(c p) o -> p c o", p=128), zcol[:, :, 0:1], C, ("raw_s",), slow=True)
        dst(raw_s[:, T + 1:T + 2].rearrange("(c p) o -> p c o", p=128), zcol[:, :, 1:2], C, ("raw_s",), slow=True)
        S.barrier()

        areset()
        wl = [take([128, 3280]) for i in range(2)]
        wc = [take([128, 3280], BF16) for i in range(2)]
        wi = 0
        for src, dstt, rows, cols in ((w_in, wbf_in, D, PCOLS), (w_out, wbf_out, D, D), (w_gate, wbf_g, D, DFF),
                                      (w_up, wbf_u, D, DFF), (w_down, wbf_d, DFF, D)):
            cw = 3280 if cols == PCOLS else (2816 if cols == DFF else 2048)
            for rb in range(rows // 128):
                for cb in range(cols // cw):
                    i = wi % 2
                    wi += 1
                    dma(wl[i][:, :cw], src[rb * 128:(rb + 1) * 128, cb * cw:(cb + 1) * cw], (), (f"wl{i}",))
                    cpa(wc[i][:, :cw], wl[i][:, :cw], (f"wl{i}",), (f"wc{i}",))
                    dst(dstt[rb * 128:(rb + 1) * 128, cb * cw:(cb + 1) * cw], wc[i][:, :cw], (f"wc{i}",), (dstt.name,))
        S.barrier()

        G = {}

        def rms_fm(src, srck, gi, dst_t, dstk):
            sqr, rstd = G["sqr"], G["rstd"]
            b = nps()
            for k in range(KC):
                i = k % 4
                act(sqr[:, i, :], src[:, k, :], AF.Square, (srck,), (f"sqr{i}",))
                mm(ps[b][:, :], ones_b[:, :], sqr[:, i, :], k == 0, k == KC - 1, (f"sqr{i}",) + C, (PK[b],))
            act(rstd, ps[b][:, :], AF.Sqrt, (PK[b],) + C, ("rstd",), bias=epsc[:, 0:1], scale=1.0 / D)
            recip(rstd, rstd, ("rstd",), ("rstd",))
            for k in range(KC):
                stt(dst_t[:, k, :], src[:, k, :], c16[:, gi, k:k + 1], rstd, ALU.mult, ALU.mult,
                    (srck, "rstd") + C, (dstk,))

        wbi = [0]

        def gemm_fm(wsrc, ncols, rhs_t, rhsk, nk, evac, col0=0, gc=512):
            wb = G["wb"]
            c = 0
            while c * 128 < ncols:
                gcols = min(gc, ncols - c * 128)
                i = wbi[0] % 2
                wbi[0] += 1
                wv = wb[i][:, 0:nk * gcols].rearrange("p (k c) -> p k c", k=nk)
                dma(wv, wsrc[:, col0 + c * 128:col0 + c * 128 + gcols].rearrange("(k p) c -> p k c", p=128),
                    (wsrc.name,), (f"wb{i}",))
                for cc in range((gcols + 127) // 128):
                    m = min(128, gcols - cc * 128)
                    b = nps()
                    for k in range(nk):
                        mm(ps[b][:m, :], wv[:, k, cc * 128:cc * 128 + m], rhs_t[:, k, :], k == 0, k == nk - 1,
                           (f"wb{i}", rhsk), (PK[b],))
                    evac(c + cc, m, b)
                c += (gcols + 127) // 128

        if last >= 1:
            areset()
            xin_t = take([128, 2, 2048])
            xTa = [take([128, KC, 512]) for i in range(2)]
            hnTa = [take([128, KC, 512], BF16) for i in range(2)]
            fsqa = take([128, 4, 512], BF16)
            rstda = take([128, 512])
            G["wb"] = [take([128, 11264], BF16) for i in range(2)]
            stg = [take([128, 512]) for i in range(4)]
            stgb = [take([128, 512], BF16) for i in range(4)]
            sti = [0]
            bmode = [None]
            fba = [0]

            def nps_a():
                if bmode[0] == "front":
                    fba[0] ^= 1
                    return 6 + fba[0]
                psi[0] = (psi[0] + 1) % 6
                return psi[0]

            def front_a(t):
                t0 = t * 512
                xT, hnT = xTa[t % 2], hnTa[t % 2]
                xk, hk = f"xT{t % 2}", f"hnT{t % 2}"
                for bb in range(2):
                    dma(xin_t, x_in[t0 + bb * 256:t0 + bb * 256 + 256, :].rearrange("(b p) d -> p b d", p=128),
                        (), ("xin",))
                    for k in range(KC):
                        b = nps_a()
                        for b2 in range(2):
                            tr(ps[b][:, b2 * 128:(b2 + 1) * 128], xin_t[:, b2, k * 128:(k + 1) * 128], ident_f[:],
                               ("xin",) + C, (PK[b],))
                        cpa(xT[:, k, bb * 256:bb * 256 + 256], ps[b][:, 0:256], (PK[b],), (xk,))
                        if k % 2 == 1:
                            yield
                dst(xT_s[:, :, t0:t0 + 512].rearrange("k p t -> p k t"), xT, (xk,), ("xT_s",))
                b = nps_a()
                for k in range(KC):
                    i = k % 4
                    act(fsqa[:, i, :], xT[:, k, :], AF.Square, (xk,), (f"fsqa{i}",))
                    mm(ps[b][:, :], ones_b[:, :], fsqa[:, i, :], k == 0, k == KC - 1, (f"fsqa{i}",) + C, (PK[b],))
                act(rstda, ps[b][:, :], AF.Sqrt, (PK[b],) + C, ("rstda",), bias=epsc[:, 0:1], scale=1.0 / D)
                recip(rstda, rstda, ("rstda",), ("rstda",))
                yield
                for k in range(KC):
                    stt(hnT[:, k, :], xT[:, k, :], c16[:, 0, k:k + 1], rstda, ALU.mult, ALU.mult, (xk, "rstda") + C, (hk,))
                    if k % 2 == 1:
                        yield

            def main_a(t):
                t0 = t * 512
                hnT, hk = hnTa[t % 2], f"hnT{t % 2}"

                def evac_a(c, b, m=128):
                    i = sti[0] % 4
                    sti[0] += 1
                    if c < 8:
                        act(stgb[i], ps[b][:, :], AF.Copy, (PK[b],), (f"stgb{i}",), scale=0.125)
                        dst(qT_s[c * 128:(c + 1) * 128, t0:t0 + 512], stgb[i], (f"stgb{i}",), ("qT_s",))
                    elif c < 16:
                        cpa(stgb[i], ps[b][:, :], (PK[b],), (f"stgb{i}",))
                        dst(kT_s[(c - 8) * 128:(c - 7) * 128, t0:t0 + 512], stgb[i], (f"stgb{i}",), ("kT_s",))
                    else:
                        cr = c - 24
                        cpa(stg[i][:m, :], ps[b][:m, :], (PK[b],), (f"stg{i}",))
                        dst(raw_s[cr * 128:cr * 128 + m, 1 + t0:1 + t0 + 512], stg[i][:m, :], (f"stg{i}",), ("raw_s",))

                def gemm_cols(col0, ncols, cbase):
                    c = 0
                    while c * 128 < ncols:
                        gcols = min(512, ncols - c * 128)
                        i = wbi[0] % 2
                        wbi[0] += 1
                        wv = G["wb"][i][:, 0:KC * gcols].rearrange("p (k c) -> p k c", k=KC)
                        dma(wv, wbf_in[:, col0 + c * 128:col0 + c * 128 + gcols].rearrange("(k p) c -> p k c", p=128),
                            ("wbf_in",), (f"wb{i}",))
                        for cc in range((gcols + 127) // 128):
                            m = min(128, gcols - cc * 128)
                            b = nps_a()
                            for k in range(KC):
                                mm(ps[b][:m, :], wv[:, k, cc * 128:cc * 128 + m], hnT[:, k, :], k == 0, k == KC - 1,
                                   (f"wb{i}", hk), (PK[b],))
                            evac_a(cbase + c + cc, b, m)
                            yield
                        c += (gcols + 127) // 128

                yield from gemm_cols(0, 2048, 0)
                for g in range(2):
                    i = wbi[0] % 2
                    wbi[0] += 1
                    wv = G["wb"][i][:, 0:KC * 512].rearrange("p (k c) -> p k c", k=KC)
                    dma(wv, wbf_in[:, 2048 + g * 512:2048 + (g + 1) * 512].rearrange("(k p) c -> p k c", p=128),
                        ("wbf_in",), (f"wb{i}",))
                    for b4 in range(4):
                        b = nps_a()
                        for k in range(KC):
                            mm(ps[b][:, :], hnT[:, k, b4 * 128:(b4 + 1) * 128], wv[:, k, :], k == 0, k == KC - 1,
                               (f"wb{i}", hk), (PK[b],))
                        j = sti[0] % 4
                        sti[0] += 1
                        cpa(stgb[j], ps[b][:, :], (PK[b],), (f"stgb{j}",))
                        dst(v_s[t0 + b4 * 128:t0 + (b4 + 1) * 128, g * 512:(g + 1) * 512], stgb[j],
                            (f"stgb{j}",), ("v_s",))
                        yield
                yield from gemm_cols(3072, PCOLS - 3072, 24)

            def run_fa(g, n):
                bmode[0] = "front"
                done = False
                for _ in range(n):
                    try:
                        next(g)
                    except StopIteration:
                        done = True
                        break
                bmode[0] = None
                return done

            run_fa(front_a(0), 10 ** 9)
            for t in range(NT):
                fg = front_a(t + 1) if t + 1 < NT else None
                for _ in main_a(t):
                    if fg is not None and run_fa(fg, 1):
                        fg = None
                if fg is not None:
                    run_fa(fg, 10 ** 9)
            S.barrier()

        if last >= 2:
            areset()
            NW = 4
            win = [take([128, 514]) for i in range(NW)]
            t1b = [take([128, 512]) for i in range(2)]
            t2b = [take([128, 512]) for i in range(2)]
            shl = take([128, 512])
            wdS = take([128, 512], BF16)
            adS = take([128, 512], BF16)
            gdS0 = take([128, 512], BF16)
            gdS1 = take([32, 512], BF16)

            def mkset(names):
                return {nm: take([128, 512]) for nm in names}
            SA, SB, FMST = [], [], []
            for q in range(2):
                d = mkset(("rS", "kS", "vS", "sg0", "sg1", "av0", "av1", "kkr", "sqk", "rsq", "kk", "tmpk0", "tmpk1",
                           "kd0", "kd1"))
                d["vb"] = take([128, 512], BF16)
                d["prr"], d["ksum"], d["gst"], d["bterm"] = d["sqk"], d["rsq"], d["tmpk0"], d["tmpk1"]
                SA.append(d)
                d2 = mkset(("lw", "cc", "cx", "dd", "di", "eIn", "eNeg", "eEx", "eHat", "kb"))
                d2["KH"], d2["BH"] = take([128, 512], BF16), take([128, 512], BF16)
                SB.append(d2)
                FMST.append([take([128, 4, 4, 128], BF16) for e in range(2)])
            AL = {"prr": "sqk", "ksum": "rsq", "gst": "tmpk0", "bterm": "tmpk1"}
            tmst = [take([128, 4, 2, 1024], BF16) for e in range(2)]
            vtmst = take([128, 4, 1024], BF16)
            gcst = take([128, 2, 8, 4])
            wi_ = [0]

            def loadwin(cr, t, m=128):
                t0 = t * 512
                i = wi_[0] % NW
                wi_[0] += 1
                w_ = win[i]
                dma(w_[:m, :], raw_s[cr * 128:cr * 128 + m, t0:t0 + 514], ("raw_s",), (f"win{i}",))
                if t0 + 512 == HT:
                    ts("dve", w_[:m, 513:514], w_[:m, 513:514], linkt[:m, 0:1], ALU.mult, (f"win{i}",) + C, (f"win{i}",))
                if t0 == HT:
                    ts("dve", w_[:m, 0:1], w_[:m, 0:1], linkt[:m, 0:1], ALU.mult, (f"win{i}",) + C, (f"win{i}",))
                return w_, f"win{i}"

            shi = [0]

            def shift(cr, t, out, outk, m=128):
                w_, wk = loadwin(cr, t, m)
                i = shi[0] % 2
                shi[0] += 1
                act(t1b[i][:m, :], w_[:m, 1:513], AF.Identity, (wk,) + C, (f"t1b{i}",), scale=cmu[:m, 2, cr:cr + 1])
                stt(t2b[i][:m, :], w_[:m, 0:512], cmu[:m, 0, cr:cr + 1], t1b[i][:m, :], ALU.mult, ALU.add,
                    (wk, f"t1b{i}") + C, (f"t2b{i}",))
                stt(out[:m, :], w_[:m, 2:514], cmu[:m, 1, cr:cr + 1], t2b[i][:m, :], ALU.mult, ALU.add,
                    (wk, f"t2b{i}") + C, (outk,))

            def v3(a):
                return a.rearrange("p (c t) -> p c t", c=4)

            def jgen(j, t, q):
                t0 = t * 512
                jc = slice(j * 128, (j + 1) * 128)
                A_, B_ = SA[q], SB[q]
                K_ = lambda nm: f"{AL.get(nm, nm)}_{q}"
                rS, kS, vS, kk = A_["rS"], A_["kS"], A_["vS"], A_["kk"]
                shift(j, t, rS, K_("rS"))
                yield
                shift(8 + j, t, kS, K_("kS"))
                yield
                shift(16 + j, t, vS, K_("vS"))
                yield
                for e in range(2):
                    er = slice(64 * e, 64 * e + 64)
                    b = nps()
                    mm(ps[b][:, :], w2b[er, jc], wdS[er, :], True, True, ("wdS",) + C, (PK[b],))
                    act(A_[f"sg{e}"], ps[b][:, :], AF.Sigmoid, (PK[b],) + C, (K_(f"sg{e}"),), bias=c8[:, 6 + e, j:j + 1])
                    b = nps()
                    mm(ps[b][:, :], a2b[er, jc], adS[er, :], True, True, ("adS",) + C, (PK[b],))
                    act(A_[f"av{e}"], ps[b][:, :], AF.Sigmoid, (PK[b],) + C, (K_(f"av{e}"),), bias=c8[:, 8 + e, j:j + 1])
                    yield
                b = nps()
                mm(ps[b][:, :], g2b0[:, jc], gdS0, True, False, ("gdS0",) + C, (PK[b],))
                mm(ps[b][:, :], g2b1[:, jc], gdS1, False, True, ("gdS1",) + C, (PK[b],))
                cp("act", A_["gst"], ps[b][:, :], (PK[b],), (K_("gst"),))
                dst(g_s[jc, t0:t0 + 512], A_["gst"], (K_("gst"),), ("g_s",))
                ts("dve", A_["kkr"], kS, c8[:, 1, j:j + 1], ALU.mult, (K_("kS"),) + C, (K_("kkr"),))
                act(A_["sqk"], A_["kkr"], AF.Square, (K_("kkr"),), (K_("sqk"),))
                yield
                b = nps()
                mm(ps[b][:, :], blk_f[:, :], A_["sqk"], True, True, (K_("sqk"),) + C, (PK[b],))
                ts("dve", A_["rsq"], ps[b][:, :], 1e-24, ALU.max, (PK[b],), (K_("rsq"),))
                act(A_["rsq"], A_["rsq"], AF.Sqrt, (K_("rsq"),), (K_("rsq"),))
                yield
                recip(A_["rsq"], A_["rsq"], (K_("rsq"),), (K_("rsq"),))
                yield
                tt("dve", kk, A_["kkr"], A_["rsq"], ALU.mult, (K_("kkr"), K_("rsq")), (K_("kk"),))
                for e in range(2):
                    ts("dve", A_[f"tmpk{e}"], A_[f"av{e}"], c8[:, 2, j:j + 1], ALU.mult, (K_(f"av{e}"),) + C,
                       (K_(f"tmpk{e}"),), omka[:, j:j + 1], ALU.add)
                    yield
                    tt("dve", A_[f"kd{e}"], A_[f"tmpk{e}"], kS, ALU.mult, (K_(f"tmpk{e}"), K_("kS")), (K_(f"kd{e}"),))
                yield
                tt("dve", A_["ksum"], A_["kd0"], A_["kd1"], ALU.add, (K_("kd0"), K_("kd1")), (K_("ksum"),))
                yield
                stt(A_["prr"], rS, c8[:, 3, j:j + 1], A_["ksum"], ALU.mult, ALU.mult, (K_("rS"), K_("ksum")) + C,
                    (K_("prr"),))
                yield
                b = nps()
                mm(ps[b][:, :], blk_f[:, :], A_["prr"], True, True, (K_("prr"),) + C, (PK[b],))
                tt("dve", A_["bterm"], ps[b][:, :], vS, ALU.mult, (PK[b], K_("vS")), (K_("bterm"),))
                dst(bt_s[jc, t0:t0 + 512], A_["bterm"], (K_("bterm"),), ("bt_s",))
                cp("act", A_["vb"], vS, (K_("vS"),), (K_("vb"),))
                yield
                lw, cc_, cx, dd, di = B_["lw"], B_["cc"], B_["cx"], B_["dd"], B_["di"]
                eIn, eNeg, eEx, eHat, kb, KH, BH = (B_[n] for n in ("eIn", "eNeg", "eEx", "eHat", "kb", "KH", "BH"))
                Q = lambda nm: f"{nm}_{q}"
                for e in range(2):
                    sg_, av_, kd_ = A_[f"sg{e}"], A_[f"av{e}"], A_[f"kd{e}"]
                    sgk, avk, kdk = K_(f"sg{e}"), K_(f"av{e}"), K_(f"kd{e}")
                    ts("dve", lw, sg_, -C0, ALU.mult, (sgk,), (Q("lw"),))
                    yield
                    S.op("dve", lambda en: en.tensor_tensor_scan(cc_, rst[:, :], lw, 0.0, ALU.mult, ALU.add),
                         (Q("lw"),) + C, (Q("cc"),))
                    yield
                    tt("dve", cx, cc_, lw, ALU.subtract, (Q("cc"), Q("lw")), (Q("cx"),))
                    tt("dve", v3(dd), bcl(v3(cc_)[:, :, 127], 128), v3(cc_), ALU.subtract, (Q("cc"),), (Q("dd"),))
                    yield
                    if e == 0:
                        Ein, Eex, Ehat = cc_, cx, dd
                        ek = (Q("cc"), Q("cx"), Q("dd"))
                    else:
                        tt("dve", di, dd, lw, ALU.add, (Q("dd"), Q("lw")), (Q("di"),))
                        Ein, Eex, Ehat = di, dd, cx
                        ek = (Q("di"), Q("dd"), Q("cx"))
                        yield
                    act(eIn, Ein, AF.Exp, (ek[0],), (Q("eIn"),))
                    act(eNeg, Ein, AF.Exp, (ek[0],), (Q("eNeg"),), scale=-1.0)
                    yield
                    act(eEx, Eex, AF.Exp, (ek[1],), (Q("eEx"),))
                    act(eHat, Ehat, AF.Exp, (ek[2],), (Q("eHat"),))
                    act(gcst[:, e, j, :], v3(cc_)[:, :, 127], AF.Exp, (Q("cc"),), ("gcst",))
                    yield
                    fs = FMST[q][e]
                    fk = f"fmst{q}{e}"
                    tt("dve", kb, kk, av_, ALU.mult, (K_("kk"), avk), (Q("kb"),))
                    tt("dve", fs[:, :, 1, :], v3(rS), v3(eIn), ALU.mult, (K_("rS"), Q("eIn")), (fk,))
                    yield
                    tt("dve", fs[:, :, 2, :], v3(kd_), v3(eNeg), ALU.mult, (kdk, Q("eNeg")), (fk,))
                    yield
                    stt(fs[:, :, 0, :], v3(kk), -1.0, v3(eEx), ALU.mult, ALU.mult, (K_("kk"), Q("eEx")), (fk,))
                    tt("dve", fs[:, :, 3, :], v3(kb), v3(eNeg), ALU.mult, (Q("kb"), Q("eNeg")), (fk,))
                    yield
                    dst(fm_s[e][j, :, t * 4:(t + 1) * 4, :, :], fs, (fk,), (f"fm_s{e}",))
                    tt("dve", KH, kd_, eHat, ALU.mult, (kdk, Q("eHat")), (Q("KH"),))
                    tt("dve", BH, kb, eHat, ALU.mult, (Q("kb"), Q("eHat")), (Q("BH"),))
                    yield
                    b = nps()
                    pb = ps[b][:, :].bitcast(BF16).rearrange("p (w b c) -> p w b c", w=2, b=4)
                    for b4 in range(4):
                        tr(pb[:, 0, b4, :], KH[:, b4 * 128:(b4 + 1) * 128], ident_b[:], (Q("KH"),) + C, (PK[b],))
                    for b4 in range(4):
                        tr(pb[:, 1, b4, :], BH[:, b4 * 128:(b4 + 1) * 128], ident_b[:], (Q("BH"),) + C, (PK[b],))
                    for w_ in range(2):
                        cp("act", tmst[e][:, :, w_, jc], pb[:, w_, :, :], (PK[b],), (f"tmst{e}",))
                    yield
                b = nps()
                pb = ps[b][:, 0:256].bitcast(BF16).rearrange("p (b c) -> p b c", b=4)
                for b4 in range(4):
                    tr(pb[:, b4, :], A_["vb"][:, b4 * 128:(b4 + 1) * 128], ident_b[:], (K_("vb"),) + C, (PK[b],))
                cp("act", vtmst[:, :, jc], pb, (PK[b],), ("vtmst",))
                yield

            for t in range(NT):
                t0 = t * 512
                shift(24, t, shl, "shl")
                act(wdS, shl, AF.Tanh, ("shl",), ("wdS",))
                shift(25, t, shl, "shl")
                cp("dve", adS, shl, ("shl",), ("adS",))
                shift(26, t, shl, "shl")
                act(gdS0, shl, AF.Sigmoid, ("shl",), ("gdS0",))
                shift(27, t, shl, "shl", m=32)
                act(gdS1, shl[:32, :], AF.Sigmoid, ("shl",), ("gdS1",))
                for jp in range(4):
                    gens = [jgen(2 * jp, t, 0), jgen(2 * jp + 1, t, 1)]
                    alive = [True, True]
                    lead = 12
                    for _ in range(lead):
                        try:
                            next(gens[0])
                        except StopIteration:
                            alive[0] = False
                            break
                    while alive[0] or alive[1]:
                        for gi in (1, 0):
                            if alive[gi]:
                                try:
                                    next(gens[gi])
                                except StopIteration:
                                    alive[gi] = False
                for e in range(2):
                    dst(tm_s[e][t * 4:(t + 1) * 4, :, :, :].rearrange("b p w c -> p b w c"), tmst[e], (f"tmst{e}",),
                        (f"tm_s{e}",))
                    dst(gC_s[e][:, :, t * 4:(t + 1) * 4], gcst[:, e, :, :], ("gcst",), (f"gC_s{e}",))
                dst(vtm_s[t * 4:(t + 1) * 4, :, :].rearrange("b p c -> p b c"), vtmst, ("vtmst",), ("vtm_s",))
            S.barrier()

        if last >= 3:
            areset()
            QT = [take([128, 2, 8, 64], BF16) for i in range(2)]
            for i in range(2):
                S.op("dve", lambda en, i=i: en.memset(QT[i], 0.0), (), (f"QT{i}",))
            KTb = [take([128, 8, 1024], BF16) for i in range(2)]
            Vb = [take([128, 8, 1024], BF16) for i in range(2)]
            tabI = take([128, 16, 4, 64], BF16)
            tabX = take([128, 16, 8, 64], BF16)
            tabF = take([128, 16, 8, 64])
            PT = [take([128, 16, 8, 64], BF16) for i in range(2)]
            rec = take([128, 8, 64])
            ost = [take([128, 8, 64]) for i in range(2)]
            dma(tabF[:, :, 0:4, :], tab_in[0], (), ("tabF",))
            cp("dve", tabI, tabF[:, :, 0:4, :], ("tabF",), ("tabI",))
            for r in range(NR):
                i = r % 2
                if RS - 3 <= r <= RS + 3:
                    kr0, nk = RS - 8, 16
                    dma(tabF, wtab_in[r - (RS - 3)], (), ("tabF",))
                    cp("dve", tabX, tabF, ("tabF",), ("tabX",))
                    tab, tabk = tabX, "tabX"
                else:
                    base = 0 if r < RS else RS
                    rl = r - base
                    kr0, nk = int(np.clip(rl - 4, 0, RS - 8)) + base, 8
                    ty = 0
                    if rl < 4:
                        ty = 1 + rl
                    elif rl > RS - 4:
                        ty = 5 + (2 - (RS - 1 - rl))
                    if ty == 0:
                        tab, tabk = tabI, "tabI"
                    else:
                        dma(tabF[:, :, 0:4, :], tab_in[ty], (), ("tabF",))
                        cp("dve", tabX[:, :, 0:4, :], tabF[:, :, 0:4, :], ("tabF",), ("tabX",))
                        tab, tabk = tabX, "tabX"
                nkc = nk // 2
                qsrc = qT_s[:, r * 64:(r + 1) * 64].rearrange("(j p) q -> p j q", p=128)
                dma(QT[i][0:64, 0, :, :], qsrc[0:64], ("qT_s",), (f"QT{i}",))
                dma(QT[i][64:128, 1, :, :], qsrc[64:128], ("qT_s",), (f"QT{i}",))
                dma(KTb[i][:, :, 0:nk * 64], kT_s[:, kr0 * 64:(kr0 + nk) * 64].rearrange("(j p) q -> p j q", p=128),
                    ("kT_s",), (f"KT{i}",))
                dma(Vb[i][:, 0:nkc, :], v_s[kr0 * 64:(kr0 + nk) * 64, :].rearrange("(k p) c -> p k c", p=128),
                    ("v_s",), (f"V{i}",))
                for h in range(16):
                    j, p = h // 2, h % 2
                    pr = slice(64 * p, 64 * p + 64)
                    if nkc == 4:
                        if h % 2 == 0:
                            b = nps()
                        off = (h % 2) * 256
                    else:
                        b = nps()
                        off = 0
                    for kc in range(nkc):
                        o_ = ps[b][:, off + kc * 64:off + (kc + 1) * 64]
                        mm(o_, KTb[i][:, j, kc * 128:(kc + 1) * 128], QT[i][:, p, j, :], True, False,
                           (f"KT{i}", f"QT{i}"), (PK[b],))
                        mm(o_, ident_b[:, :], tab[:, h, kc, :], False, True, (tabk,) + C, (PK[b],))
                    if nkc == 8 or h % 2 == 1:
                        if nkc == 4:
                            act(PT[i][:, h - 1:h + 1, 0:4, :], ps[b][:, :].rearrange("p (h k c) -> p h k c", h=2, k=4),
                                AF.Exp, (PK[b],), (f"PT{i}",))
                        else:
                            act(PT[i][:, h, :, :], ps[b][:, :].rearrange("p (k c) -> p k c", k=8), AF.Exp, (PK[b],),
                                (f"PT{i}",))
                ba, bb_ = nps(), nps()
                for h in range(16):
                    j, p = h // 2, h % 2
                    pr = slice(64 * p, 64 * p + 64)
                    b = ba if j < 4 else bb_
                    o0 = (j % 4) * 128
                    for kc in range(nkc):
                        mm(ps[b][pr, o0:o0 + 64], Vb[i][:, kc, h * 64:(h + 1) * 64], PT[i][:, h, kc, :], kc == 0,
                           kc == nkc - 1, (f"V{i}", f"PT{i}"), (PK[b],))
                    for kc in range(nkc):
                        mm(ps[b][pr, o0 + 64:o0 + 128], ones_b[:, 0:64], PT[i][:, h, kc, :], kc == 0, kc == nkc - 1,
                           (f"PT{i}",) + C, (PK[b],))
                for half, b in ((0, ba), (1, bb_)):
                    pv = ps[b][:, :].rearrange("p (j x c) -> p j x c", j=4, x=2)
                    recip(rec[:, half * 4:(half + 1) * 4, :], pv[:, :, 1, :], (PK[b],), ("rec",))
                    tt("dve", ost[i][:, half * 4:(half + 1) * 4, :], pv[:, :, 0, :], rec[:, half * 4:(half + 1) * 4, :],
                       ALU.mult, (PK[b], "rec"), (f"ost{i}",))
                dst(attT_s[:, r * 64:(r + 1) * 64].rearrange("(j p) q -> p j q", p=128), ost[i], (f"ost{i}",), ("attT_s",))
            S.barrier()

        if last >= 4:
            areset()
            FMt = [[take([128, 8, 4, 128], BF16) for i in range(2)] for e in range(2)]
            TMt = [[take([128, 2, 1024], BF16) for i in range(2)] for e in range(2)]
            Vt = [[take([128, 1024], BF16) for i in range(2)] for e in range(2)]
            gCt = [take([128, 8, NB]) for e in range(2)]
            Hs = [take([128, 8, 64]) for e in range(2)]
            Hb = [take([128, 8, 2, 64], BF16) for e in range(2)]
            ZRz = [take([128, 8, 2, 256], BF16) for e in range(2)]
            AT = [[take([128, 8, 512], BF16) for e in range(2)] for u in range(2)]
            PQ = [[[take([128, 8, 256], BF16) for i in range(2)] for e in range(2)] for u in range(2)]
            Xb = [[[take([128, 8, 128], BF16) for i in range(2)] for e in range(2)] for u in range(2)]
            Wsb = [take([128, 8, 64], BF16) for e in range(2)]
            Usb = [take([128, 8, 64], BF16) for e in range(2)]
            Yst = [[take([128, 4, 128]) for e in range(2)] for u in range(2)]
            for e in range(2):
                dma(gCt[e], gC_s[e][:, :, :], (f"gC_s{e}",), (f"gCt{e}",))
                S.op("dve", lambda en, e=e: en.memset(Hs[e], 0.0), (), (f"H{e}",))
                S.op("dve", lambda en, e=e: en.memset(Hb[e], 0.0), (), (f"Hb{e}",))

            def s1_parts(step, hf):
                si, up = step % 2, hf
                cis = (step, NB - 1 - step)
                parts = []

                def pA1():
                    if hf == 0:
                        for e in range(2):
                            ci = cis[e]
                            dma(FMt[e][si], fm_s[e][:, :, ci, :, :].rearrange("j p w t -> p j w t"), (f"fm_s{e}",),
                                (f"FM{e}{si}",))
                            dma(TMt[e][si], tm_s[e][ci], (f"tm_s{e}",), (f"TM{e}{si}",))
                            dma(Vt[e][si], vtm_s[ci], ("vtm_s",), (f"V{e}{si}",))
                    for e in range(2):
                        fm, fk, zk = FMt[e][si], f"FM{e}{si}", f"ZRz{e}"
                        if hf == 0:
                            zsrc = fm[:, :, 0:2, :].rearrange("p j w t -> p j (w t)")
                            ts("dve", ZRz[e][:, :, 0, :], zsrc, blk_f[:, 0:1], ALU.mult, (fk,) + C, (zk,))
                            act(ZRz[e][:, :, 1, :], zsrc, AF.Identity, (fk,) + C, (zk,), scale=blk_f[:, 64:65])
                        for hh in range(8):
                            h = hf * 8 + hh
                            j, p = h // 2, h % 2
                            b = nps()
                            mm(ps[b][:, 0:256], fm[:, j, 2, :], ZRz[e][:, j, p, :], True, True, (fk, zk), (PK[b],))
                            mm(ps[b][:, 256:512], fm[:, j, 3, :], ZRz[e][:, j, p, :], True, True, (fk, zk), (PK[b],))
                            tt("dve", AT[up][e][:, hh, :], ps[b][:, :], msk4[:, e, :], ALU.mult, (PK[b],) + C,
                               (f"AT{up}{e}",))
                parts.append(pA1)

                def pA2():
                    for e in range(2):
                        fm, fk, zk = FMt[e][si], f"FM{e}{si}", f"ZRz{e}"
                        for g in range(2):
                            b = nps()
                            for h4 in range(4):
                                hh = g * 4 + h4
                                h = hf * 8 + hh
                                j, p = h // 2, h % 2
                                mm(ps[b][:, h4 * 128:(h4 + 1) * 128], ZRz[e][:, j, p, 0:128], fm[:, j, 3, :], True, True,
                                   (fk, zk), (PK[b],))
                            tt("dve", PQ[up][e][0][:, g * 4:(g + 1) * 4, 128:256],
                               ps[b][:, :].rearrange("p (h c) -> p h c", h=4), bc(mskl[:, e, :], 4), ALU.mult,
                               (PK[b],) + C, (f"PQ{up}{e}0g{2 * g}", f"PQ{up}{e}0g{2 * g + 1}"))
                        cp("act", PQ[up][e][0][:, :, 0:128], AT[up][e][:, :, 256:384], (f"AT{up}{e}",),
                           tuple(f"PQ{up}{e}0g{g}" for g in range(4)))
                        tt("dve", Xb[up][e][0][:, :, :], AT[up][e][:, :, 256:384], bc(ident_b[:, :], 8), ALU.add,
                           (f"AT{up}{e}",) + C, (f"X{up}{e}0x0", f"X{up}{e}0x1"))
                parts.append(pA2)

                def mk_level(lv):
                    def pL():
                        cur, nxt = lv % 2, (lv + 1) % 2
                        for e in range(2):
                            for g in range(4):
                                pk_c, pk_n = f"PQ{up}{e}{cur}g{g}", f"PQ{up}{e}{nxt}g{g}"
                                b = nps()
                                for h2 in range(2):
                                    hh = g * 2 + h2
                                    Pc, Qc = PQ[up][e][cur][:, hh, 0:128], PQ[up][e][cur][:, hh, 128:256]
                                    if lv < 5:
                                        mm(ps[b][:, h2 * 256:h2 * 256 + 128], Qc, Pc, True, True, (pk_c,), (PK[b],))
                                    mm(ps[b][:, h2 * 256 + 128:h2 * 256 + 256], Pc, Qc, True, True, (pk_c,), (PK[b],))
                                if lv < 5:
                                    cpa(PQ[up][e][nxt][:, g * 2:g * 2 + 2, :],
                                        ps[b][:, :].rearrange("p (h c) -> p h c", h=2), (PK[b],), (pk_n,))
                                else:
                                    cpa(PQ[up][e][nxt][:, g * 2:g * 2 + 2, 128:256],
                                        ps[b][:, :].rearrange("p (h c) -> p h c", h=2)[:, :, 128:256], (PK[b],), (pk_n,))
                        for e in range(2):
                            for g in range(2):
                                xk_c, xk_n = f"X{up}{e}{cur}x{g}", f"X{up}{e}{nxt}x{g}"
                                b = nps()
                                for h4 in range(4):
                                    hh = g * 4 + h4
                                    o_ = ps[b][:, h4 * 128:(h4 + 1) * 128]
                                    mm(o_, ident_b[:, :], Xb[up][e][cur][:, hh, :], True, False, (xk_c,) + C, (PK[b],))
                                    mm(o_, PQ[up][e][nxt][:, hh, 128:256], Xb[up][e][cur][:, hh, :], False, True,
                                       (f"PQ{up}{e}{nxt}g{hh // 2}", xk_c), (PK[b],))
                                cpa(Xb[up][e][nxt][:, g * 4:(g + 1) * 4, :],
                                    ps[b][:, :].rearrange("p (h c) -> p h c", h=4), (PK[b],), (xk_n,))
                    return pL
                for lv in range(6):
                    parts.append(mk_level(lv))
                return parts

            def s2_parts(step, hf):
                si, up = step % 2, hf
                cis = (step, NB - 1 - step)
                XF = 0

                def ctx(e):
                    return (FMt[e][si], f"FM{e}{si}", TMt[e][si], f"TM{e}{si}", Vt[e][si], f"V{e}{si}",
                            (f"X{up}{e}{XF}x0", f"X{up}{e}{XF}x1"), f"AT{up}{e}")

                def pW():
                    for e in range(2):
                        fm, fk, tm, tk, vt, vk, xk, ak = ctx(e)
                        b = nps()
                        for hh in range(8):
                            h = hf * 8 + hh
                            j, p = h // 2, h % 2
                            o_ = ps[b][:, hh * 64:(hh + 1) * 64]
                            mm(o_, fm[:, j, 0, :], Hb[e][:, j, p, :], True, False, (fk, f"Hb{e}"), (PK[b],))
                            mm(o_, AT[up][e][:, hh, 0:128], vt[:, h * 64:(h + 1) * 64], False, True, (ak, vk), (PK[b],))
                        cpa(Wsb[e][:, :, :], ps[b][:, :].rearrange("p (h c) -> p h c", h=8), (PK[b],), (f"W{e}",))

                def pU():
                    for e in range(2):
                        fm, fk, tm, tk, vt, vk, xk, ak = ctx(e)
                        b = nps()
                        for hh in range(8):
                            mm(ps[b][:, hh * 64:(hh + 1) * 64], Xb[up][e][XF][:, hh, :], Wsb[e][:, hh, :], True, True,
                               xk + (f"W{e}",), (PK[b],))
                        cpa(Usb[e][:, :, :], ps[b][:, :].rearrange("p (h c) -> p h c", h=8), (PK[b],), (f"U{e}",))

                def pY():
                    for e in range(2):
                        ci = cis[e]
                        fm, fk, tm, tk, vt, vk, xk, ak = ctx(e)
                        b = nps()
                        for hh in range(8):
                            h = hf * 8 + hh
                            j, p = h // 2, h % 2
                            pr = slice(64 * p, 64 * p + 64)
                            o_ = ps[b][pr, (j % 4) * 128:(j % 4 + 1) * 128]
                            mm(o_, Hb[e][:, j, p, :], fm[:, j, 1, :], True, False, (fk, f"Hb{e}"), (PK[b],))
                            mm(o_, Usb[e][:, hh, :], AT[up][e][:, hh, 384:512], False, False, (f"U{e}", ak), (PK[b],))
                            mm(o_, vt[:, h * 64:(h + 1) * 64], AT[up][e][:, hh, 128:256], False, True, (vk, ak), (PK[b],))
                        cpa(Yst[up][e][:, :, :], ps[b][:, :].rearrange("p (j c) -> p j c", j=4), (PK[b],), (f"Yst{up}{e}",))
                        dst(y_s[e][hf * 512:(hf + 1) * 512, ci * 128:(ci + 1) * 128].rearrange("(j p) t -> p j t", p=128),
                            Yst[up][e], (f"Yst{up}{e}",), (f"y_s{e}",))

                def pH():
                    for e in range(2):
                        ci = cis[e]
                        fm, fk, tm, tk, vt, vk, xk, ak = ctx(e)
                        b = nps()
                        for hh in range(8):
                            h = hf * 8 + hh
                            j, p = h // 2, h % 2
                            pr = slice(64 * p, 64 * p + 64)
                            o_ = ps[b][pr, (j % 4) * 64:(j % 4 + 1) * 64]
                            mm(o_, tm[:, 1, h * 64:(h + 1) * 64], Usb[e][:, hh, :], True, False, (tk, f"U{e}"), (PK[b],))
                            mm(o_, tm[:, 0, h * 64:(h + 1) * 64], vt[:, h * 64:(h + 1) * 64], False, True, (tk, vk),
                               (PK[b],))
                        js = slice(hf * 4, (hf + 1) * 4)
                        tt("dve", Hs[e][:, js, :], Hs[e][:, js, :], bcl(gCt[e][:, js, ci], 64), ALU.mult,
                           (f"H{e}", f"gCt{e}"), (f"H{e}",))
                        tt("dve", Hs[e][:, js, :], Hs[e][:, js, :], ps[b][:, 0:256].rearrange("p (j c) -> p j c", j=4),
                           ALU.add, (f"H{e}", PK[b]), (f"H{e}",))
                        if (e == 0 and ci == NB // 2 - 1) or (e == 1 and ci == NB // 2):
                            ts("dve", Hs[e][:, js, :], Hs[e][:, js, :], linkt[:, 0:1], ALU.mult, (f"H{e}",) + C, (f"H{e}",))
                        ts("dve", Hb[e][:, js, 0, :], Hs[e][:, js, :], blk_f[:, 0:1], ALU.mult, (f"H{e}",) + C, (f"Hb{e}",))
                        act(Hb[e][:, js, 1, :], Hs[e][:, js, :], AF.Identity, (f"H{e}",) + C, (f"Hb{e}",),
                            scale=blk_f[:, 64:65])
                return [pW, pU, pY, pH]

            cpr[0] = 3
            units = [(st, hf) for st in range(NB) for hf in range(2)]
            prev2 = None
            for u in units + [None]:
                p1 = s1_parts(*u) if u is not None else []
                p2 = prev2 if prev2 is not None else []
                order = []
                i1 = i2 = 0
                while i1 < len(p1) or i2 < len(p2):
                    if i1 < len(p1):
                        order.append(p1[i1]); i1 += 1
                    if i2 < len(p2):
                        order.append(p2[i2]); i2 += 1
                for f_ in order:
                    f_()
                prev2 = s2_parts(*u) if u is not None else None
            cpr[0] = 2
            S.barrier()

        if last >= 5:
            areset()
            xT = take([128, KC, 512])
            cat = [take([128, KC, 512], BF16) for i in range(2)]
            G["sqr"] = take([128, 4, 512], BF16)
            G["rstd"] = take([128, 512])
            ffT = take([128, FC, 512], BF16)
            G["wb"] = [take([128, 11264], BF16) for i in range(2)]
            ld = []
            for q_ in range(5):
                if q_ < 4:
                    b_ = take([128, 512])
                    ld.append([b_, b_])
                else:
                    ld.append([take([128, 512]) for i in range(2)])
            ysum, ym, sq2, rs2 = take([128, 512]), take([128, 512]), take([128, 512]), take([128, 512])
            sil = take([128, 512])
            fsq = take([128, 2, 512], BF16)
            outF = ffT[:, 0:32, :].rearrange("p a b -> p (a b)").bitcast(F32).rearrange("p (k t) -> p k t", k=KC)
            bank_mode = [None]
            fb = [0]

            def nps_d():
                if bank_mode[0] == "front":
                    fb[0] ^= 1
                    return 6 + fb[0]
                psi[0] = (psi[0] + 1) % 6
                return psi[0]

            def front(t):
                t0 = t * 512
                ct, ck = cat[t % 2], f"cat{t % 2}"
                ba = nps_d()
                for j in range(8):
                    i = j % 2
                    dma(ld[4][i], attT_s[j * 128:(j + 1) * 128, t0:t0 + 512], ("attT_s",), (f"ld4{i}",))
                    act(fsq[:, i, :], ld[4][i], AF.Square, (f"ld4{i}",), (f"fsq{i}",))
                    mm(ps[ba][:, :], ones_b[:, :], fsq[:, i, :], j == 0, j == 7, (f"fsq{i}",) + C, (PK[ba],))
                act(rs2, ps[ba][:, :], AF.Sqrt, (PK[ba],) + C, ("rs2",), bias=epsc[:, 0:1], scale=1.0 / 1024)
                recip(rs2, rs2, ("rs2",), ("rs2",))
                yield
                for j in range(8):
                    i = j % 2
                    dma(ld[4][i], attT_s[j * 128:(j + 1) * 128, t0:t0 + 512], ("attT_s",), (f"ld4{i}",))
                    stt(ct[:, j, :], ld[4][i], c8[:, 0, j:j + 1], rs2, ALU.mult, ALU.mult, (f"ld4{i}", "rs2") + C, (ck,))
                    yield
                for j in range(8):
                    i = j % 2
                    jc = slice(j * 128, (j + 1) * 128)
                    dma(ld[0][i], y_s[0][jc, t0:t0 + 512], ("y_s0",), ("ld0",))
                    dma(ld[1][i], y_s[1][jc, t0:t0 + 512], ("y_s1",), ("ld1",))
                    dma(ld[2][i], bt_s[jc, t0:t0 + 512], ("bt_s",), ("ld2",))
                    dma(ld[3][i], g_s[jc, t0:t0 + 512], ("g_s",), ("ld3",))
                    tt("dve", ysum, ld[0][i], ld[1][i], ALU.add, ("ld0", "ld1"), ("ysum",))
                    b = nps_d()
                    mm(ps[b][:, :], blk_f[:, :], ysum, True, True, ("ysum",) + C, (PK[b],))
                    yield
                    stt(ym, ps[b][:, :], -1.0 / 64, ysum, ALU.mult, ALU.add, (PK[b], "ysum"), ("ym",))
                    act(sq2, ym, AF.Square, ("ym",), ("sq2",))
                    b = nps_d()
                    mm(ps[b][:, :], blk_f[:, :], sq2, True, True, ("sq2",) + C, (PK[b],))
                    yield
                    act(rs2, ps[b][:, :], AF.Sqrt, (PK[b],) + C, ("rs2",), bias=epsc[:, 1:2], scale=1.0 / 64)
                    recip(rs2, rs2, ("rs2",), ("rs2",))
                    yield
                    tt("dve", ym, ym, rs2, ALU.mult, ("ym", "rs2"), ("ym",))
                    ts("dve", ym, ym, c8[:, 4, j:j + 1], ALU.mult, ("ym",) + C, ("ym",), c8[:, 5, j:j + 1], ALU.add)
                    yield
                    tt("dve", ym, ym, ld[2][i], ALU.add, ("ym", "ld2"), ("ym",))
                    tt("dve", ct[:, 8 + j, :], ym, ld[3][i], ALU.mult, ("ym", "ld3"), (ck,))
                    yield

            def gemm_gen(wsrc, ncols, rhs_t, rhsk, nk, evac, gc=512):
                wb = G["wb"]
                c = 0
                while c * 128 < ncols:
                    gcols = min(gc, ncols - c * 128)
                    i = wbi[0] % 2
                    wbi[0] += 1
                    wv = wb[i][:, 0:nk * gcols].rearrange("p (k c) -> p k c", k=nk)
                    dma(wv, wsrc[:, c * 128:c * 128 + gcols].rearrange("(k p) c -> p k c", p=128),
                        (wsrc.name,), (f"wb{i}",))
                    for cc in range((gcols + 127) // 128):
                        b = nps_d()
                        for k in range(nk):
                            mm(ps[b][:, :], wv[:, k, cc * 128:cc * 128 + 128], rhs_t[:, k, :], k == 0, k == nk - 1,
                               (f"wb{i}", rhsk), (PK[b],))
                        evac(c + cc, 128, b)
                        yield
                    c += (gcols + 127) // 128

            def rms_d(src, srck, gi, dst_t, dstk):
                sqr, rstd = G["sqr"], G["rstd"]
                b = nps_d()
                for k in range(KC):
                    i = k % 4
                    act(sqr[:, i, :], src[:, k, :], AF.Square, (srck,), (f"sqr{i}",))
                    mm(ps[b][:, :], ones_b[:, :], sqr[:, i, :], k == 0, k == KC - 1, (f"sqr{i}",) + C, (PK[b],))
                act(rstd, ps[b][:, :], AF.Sqrt, (PK[b],) + C, ("rstd",), bias=epsc[:, 0:1], scale=1.0 / D)
                recip(rstd, rstd, ("rstd",), ("rstd",))
                for k in range(KC):
                    stt(dst_t[:, k, :], src[:, k, :], c16[:, gi, k:k + 1], rstd, ALU.mult, ALU.mult,
                        (srck, "rstd") + C, (dstk,))

            def main(t):
                t0 = t * 512
                ct, ck = cat[t % 2], f"cat{t % 2}"
                ostD = ct.rearrange("p a b -> p (a b)").bitcast(F32).rearrange("p (i d) -> p i d", i=2)
                dma(xT, xT_s[:, :, t0:t0 + 512].rearrange("k p t -> p k t"), ("xT_s",), ("xT",))

                def evac_x(c, m, b):
                    tt("dve", xT[:, c, :], xT[:, c, :], ps[b][:, :], ALU.add, ("xT", PK[b]), ("xT",))

                yield from gemm_gen(wbf_out, D, ct, ck, KC, evac_x)
                rms_d(xT, "xT", 1, ct, ck)
                yield
                for f in range(FC // 2):
                    i = wbi[0] % 2
                    wbi[0] += 1
                    wv = G["wb"][i][:, 0:KC * 512].rearrange("p (k c) -> p k c", k=KC)
                    dma(wv[:, :, 0:256], wbf_g[:, f * 256:(f + 1) * 256].rearrange("(k p) c -> p k c", p=128),
                        ("wbf_g",), (f"wb{i}",))
                    dma(wv[:, :, 256:512], wbf_u[:, f * 256:(f + 1) * 256].rearrange("(k p) c -> p k c", p=128),
                        ("wbf_u",), (f"wb{i}",))
                    for fc in range(2):
                        bg, bu = nps_d(), nps_d()
                        for k in range(KC):
                            mm(ps[bg][:, :], wv[:, k, fc * 128:(fc + 1) * 128], ct[:, k, :], k == 0, k == KC - 1,
                               (f"wb{i}", ck), (PK[bg],))
                        for k in range(KC):
                            mm(ps[bu][:, :], wv[:, k, 256 + fc * 128:256 + (fc + 1) * 128], ct[:, k, :], k == 0,
                               k == KC - 1, (f"wb{i}", ck), (PK[bu],))
                        act(sil, ps[bg][:, :], AF.Silu, (PK[bg],), ("sil",))
                        tt("dve", ffT[:, f * 2 + fc, :], sil, ps[bu][:, :], ALU.mult, ("sil", PK[bu]), ("ffT",))
                        yield
                yield from gemm_gen(wbf_d, D, ffT, "ffT", FC, evac_x, gc=256)
                rms_d(xT, "xT", 2, outF, "ffT")
                yield
                for b4 in range(4):
                    i = b4 % 2
                    for kq in range(4):
                        b = nps_d()
                        for k4 in range(4):
                            k = kq * 4 + k4
                            tr(ps[b][:, k4 * 128:(k4 + 1) * 128], outF[:, k, b4 * 128:(b4 + 1) * 128], ident_f[:],
                               ("ffT",) + C, (PK[b],))
                        cpa(ostD[:, i, kq * 512:(kq + 1) * 512], ps[b][:, :], (PK[b],), (ck,))
                    dst(y_out[t0 + b4 * 128:t0 + (b4 + 1) * 128, :], ostD[:, i, :], (ck,), ("y_out",))
                    yield

            def run_front(g, n):
                bank_mode[0] = "front"
                done = False
                for _ in range(n):
                    try:
                        next(g)
                    except StopIteration:
                        done = True
                        break
                bank_mode[0] = None
                return done

            run_front(front(0), 10 ** 9)
            for t in range(NT):
                fg = front(t + 1) if t + 1 < NT else None
                nm = 0
                for _ in main(t):
                    nm += 1
                    if fg is not None and nm > 16:
                        if run_front(fg, 2):
                            fg = None
                if fg is not None:
                    run_front(fg, 10 ** 9)
            S.barrier()

        blk = es.enter_context(nc.Block())

        def mk(e):
            def body(eng):
                S.replay(e, eng)
            return body

        blk.tensor(mk("pe"))
        blk.scalar(mk("act"))
        blk.vector(mk("dve"))
        blk.gpsimd(mk("pool"))
        blk.sync(mk("sp"))
    return nc


def _consts():
    p = np.arange(128)[:, None]
    f = np.arange(128)[None, :]
    ident = (p == f)
    msk = np.stack([np.stack([f > p, f >= p], 0), np.stack([f < p, f <= p], 0)], 0)
    msk = np.transpose(msk, (2, 0, 1, 3)).reshape(128, 512)
    mskl = np.stack([f < p, f > p], 0)
    mskl = np.transpose(mskl, (1, 0, 2)).reshape(128, 256)
    blk = ((p >= 64) == (f >= 64))
    rst = np.ones((128, 512), np.float32)
    rst[:, ::128] = 0.0
    return np.concatenate([ident, msk, mskl, blk, rst], axis=1).astype(np.float32)


def _pcol(v, k):
    return np.ascontiguousarray(np.asarray(v, np.float32).reshape(k, 128).T)


NEG = -30000.0


def _tables(rpb, RS):
    H = rpb.shape[0]
    c = np.arange(64)
    cs = np.clip(c - 8, 0, 48)
    cp_ = np.arange(64)[:, None]
    valid = (cp_ >= cs[None, :]) & (cp_ < cs[None, :] + 16)
    dcol = np.clip(cp_ - c[None, :] + 15, 0, 30)
    def tile_for(drows):
        nk = len(drows)
        out = np.full((nk, 64, H, 64), NEG, np.float32)
        for j, dr in enumerate(drows):
            if dr is None or dr < 0 or dr > 14:
                continue
            b = rpb[:, dr, :][:, dcol]
            b = np.where(valid[None], b, NEG)
            out[j] = np.transpose(b, (1, 0, 2))
        out = out.reshape(nk // 2, 2 * 64, H, 64)
        return np.transpose(out, (1, 2, 0, 3))
    types = [[j + 3 for j in range(8)]]
    for r in range(4):
        types.append([j - r + 7 for j in range(8)])
    for d in (2, 1, 0):
        types.append([j + d for j in range(8)])
    tab = np.stack([tile_for(t) for t in types], 0).astype(np.float32)
    return tab, tile_for


def _wide_tables(tile_for, RS, link):
    out = []
    R2 = 2 * RS
    for r in range(RS - 3, RS + 4):
        if link:
            r0 = int(np.clip(r - 4, 0, R2 - 8))
        else:
            base = 0 if r < RS else RS
            r0 = int(np.clip(r - 4, base, base + RS - 8))
        drows = []
        for jj in range(16):
            kr = RS - 8 + jj
            if r0 <= kr < r0 + 8:
                drows.append(kr - r + 7)
            else:
                drows.append(None)
        out.append(tile_for(drows))
    return np.stack(out, 0).astype(np.float32)


def make_in_map(xslot, link, P, RS):
    f = np.float32
    mu = np.zeros((2, 28 * 128), f)
    mu[0, :3488] = P["mu_prev"].reshape(-1)
    mu[1, :3488] = P["mu_next"].reshape(-1)
    mu = np.stack([_pcol(mu[0], 28), _pcol(mu[1], 28)], 1)
    pk16 = np.stack([_pcol(P["norm1_g"].reshape(-1), 16), _pcol(P["norm2_g"].reshape(-1), 16),
                     _pcol(P["final_g"].reshape(-1), 16)], 1)
    w0 = P["w0"].reshape(2, 1024)
    a0 = P["a0"].reshape(2, 1024)
    pk8 = np.stack([_pcol(P[k].reshape(-1), 8) for k in ("attn_out_g", "k_k", "k_a", "r_k", "lnx_w", "lnx_b")]
                   + [_pcol(w0[0], 8), _pcol(w0[1], 8), _pcol(a0[0], 8), _pcol(a0[1], 8)], 1)
    tab, tile_for = _tables(np.asarray(P["attn_rpb"], f)[0], RS)
    wtab = _wide_tables(tile_for, RS, link)
    return {
        "x": np.ascontiguousarray(xslot, dtype=f),
        "w_in": np.ascontiguousarray(P["w_in"][0], dtype=f), "w_out": np.ascontiguousarray(P["w_out"][0], dtype=f),
        "w_gate": np.ascontiguousarray(P["w_gate"][0], dtype=f), "w_up": np.ascontiguousarray(P["w_up"][0], dtype=f),
        "w_down": np.ascontiguousarray(P["w_down"][0], dtype=f),
        "pk16": np.ascontiguousarray(pk16), "pk8": np.ascontiguousarray(pk8), "mu": np.ascontiguousarray(mu),
        "w2": np.ascontiguousarray(np.asarray(P["w2"], f).reshape(128, 1024)),
        "a2": np.ascontiguousarray(np.asarray(P["a2"], f).reshape(128, 1024)),
        "g2": np.ascontiguousarray(np.asarray(P["g2"], f).reshape(160, 1024)),
        "link": np.full((128, 1), float(link), f), "cst": _consts(),
        "tab": np.ascontiguousarray(tab), "wtab": np.ascontiguousarray(wtab),
    }


_NC_CACHE = {}


def kernel(**inputs):
    RS = 128
    P = {k: np.asarray(v) for k, v in inputs.items() if not k.startswith("x_")}
    xp = np.asarray(inputs["x_prompt"], dtype=np.float32)
    xs = np.asarray(inputs["x_sample"], dtype=np.float32)
    T = 2 * RS * 64
    slots = [(xp[0:2].reshape(T, D), 0), (xp[2:4].reshape(T, D), 0), (xs[0].reshape(T, D), 1)]
    if RS not in _NC_CACHE:
        _NC_CACHE[RS] = build(RS)
    nc = _NC_CACHE[RS]
    real = [make_in_map(x, link, P, RS) for x, link in slots]
    dummy = dict(real[0])
    dummy["x"] = np.zeros((T, D), np.float32)
    place = {0: 0, 1: 1, 4: 2}
    maps = [real[place[c]] if c in place else dummy for c in range(NCORES)]
    res = run_bass_kernel_spmd(nc, maps, core_ids=list(range(NCORES)))
    ys = [np.asarray(res.results[c]["y"], dtype=np.float32) for c in (0, 1, 4)]
    y_prompt = np.concatenate([ys[0].reshape(2, T // 2, D), ys[1].reshape(2, T // 2, D)], axis=0)
    y_sample = ys[2].reshape(1, T, D)
    return (y_prompt, y_sample)
```

```python
import numpy as np
import concourse.bass as bass
import concourse.mybir as mybir
from concourse.bass_utils import run_bass_kernel_spmd

F32 = mybir.dt.float32
BF16 = mybir.dt.bfloat16
AF = mybir.ActivationFunctionType
ALU = mybir.AluOpType

D = 2048
KC = 16
PCOLS = 6560
DFF = 5632
FC = 44
NCORES = 8
C0 = float(np.exp(-0.5))


class Sched:
    ENG = ("pe", "act", "dve", "pool", "sp")

    def __init__(self, esems, dsems):
        self.q = {e: [] for e in self.ENG}
        self.cnt = {e: 0 for e in self.ENG}
        self.seen = {e: {} for e in self.ENG}
        self.buf = {}
        self.esems = esems
        self.dsems = dsems
        self.dused = [0] * len(dsems)
        self.drr = 0

    def _need(self, e, key, v, waits):
        if key[0] == "e" and key[1] == e and e == "pe":
            return
        if self.seen[e].get(key, 0) >= v:
            return
        self.seen[e][key] = v
        waits.append((key, v))

    def op(self, e, fn, r=(), w=(), dma=False):
        waits = []
        for k in r:
            b = self.buf.get(k)
            if b:
                for key, v in b[0].items():
                    self._need(e, key, v, waits)
        for k in w:
            b = self.buf.get(k)
            if b:
                for key, v in b[0].items():
                    self._need(e, key, v, waits)
                for key, v in b[1].items():
                    self._need(e, key, v, waits)
        if dma:
            j = self.drr
            self.drr = (self.drr + 1) % len(self.dsems)
            if self.dused[j] > 0:
                self._need(e, ("d", j), 16 * self.dused[j], waits)
            self.dused[j] += 1
            key, v = ("d", j), 16 * self.dused[j]
        else:
            self.cnt[e] += 1
            key, v = ("e", e), self.cnt[e]
        for k in r:
            b = self.buf.setdefault(k, [{}, {}])
            b[1][key] = max(b[1].get(key, 0), v)
        for k in w:
            b = self.buf.setdefault(k, [{}, {}])
            b[1].clear()
            b[0][key] = max(b[0].get(key, 0), v)
        self.q[e].append((waits, fn, key, v))

    def sem(self, key):
        return self.esems[key[1]] if key[0] == "e" else self.dsems[key[1]]

    def replay(self, e, eng):
        for waits, fn, key, v in self.q[e]:
            for wk, wv in waits:
                eng.wait_ge(self.sem(wk), wv)
            if fn is None:
                continue
            ins = fn(eng)
            ins.then_inc(self.sem(key), 16 if key[0] == "d" else 1)

    def barrier(self):
        snap = [(("e", p), self.cnt[p]) for p in self.ENG if self.cnt[p]]
        snap += [(("d", j), 16 * u) for j, u in enumerate(self.dused) if u]
        for e in self.ENG:
            waits = []
            for key, v in snap:
                if key[0] == "e" and key[1] == e and e == "pe":
                    pass
                if self.seen[e].get(key, 0) >= v:
                    continue
                self.seen[e][key] = v
                waits.append((key, v))
            self.q[e].append((waits, None, None, None))
        self.buf.clear()

    def finish(self, e="sp"):
        waits = []
        for j, u in enumerate(self.dused):
            if u:
                self._need(e, ("d", j), 16 * u, waits)
        return waits


def build(RS, dbg=(), stop_after="D"):
    T = 2 * RS * 64
    NT = T // 512
    NB = T // 128
    HT = T // 2
    NR = 2 * RS
    nc = bass.Bass("TRN2", target_bir_lowering=False)
    dbg = set(dbg)
    PH = "0A2BCD"
    last = PH.index(stop_after)

    def din(name, shape, dt=F32):
        return nc.dram_tensor(name, list(shape), dt, kind="ExternalInput")

    def dsc(name, shape, dt):
        return nc.dram_tensor(name, list(shape), dt, kind=("ExternalOutput" if name in dbg else "Internal"))

    x_in = din("x", [T, D])
    w_in = din("w_in", [D, PCOLS])
    w_out = din("w_out", [D, D])
    w_gate = din("w_gate", [D, DFF])
    w_up = din("w_up", [D, DFF])
    w_down = din("w_down", [DFF, D])
    pk16 = din("pk16", [128, 3, 16])
    pk8 = din("pk8", [128, 10, 8])
    mu_in = din("mu", [128, 2, 28])
    w2_in = din("w2", [128, 1024])
    a2_in = din("a2", [128, 1024])
    g2_in = din("g2", [160, 1024])
    link_in = din("link", [128, 1])
    cst_in = din("cst", [128, 1536])
    tab_in = din("tab", [8, 128, 16, 4, 64])
    wtab_in = din("wtab", [7, 128, 16, 8, 64])
    y_out = nc.dram_tensor("y", [T, D], F32, kind="ExternalOutput")

    wbf_in = dsc("wbf_in", [D, PCOLS], BF16)
    wbf_out = dsc("wbf_out", [D, D], BF16)
    wbf_g = dsc("wbf_g", [D, DFF], BF16)
    wbf_u = dsc("wbf_u", [D, DFF], BF16)
    wbf_d = dsc("wbf_d", [DFF, D], BF16)
    xT_s = dsc("xT_s", [KC, 128, T], F32)
    qT_s = dsc("qT_s", [1024, T], BF16)
    kT_s = dsc("kT_s", [1024, T], BF16)
    v_s = dsc("v_s", [T, 1024], BF16)
    raw_s = dsc("raw_s", [28 * 128, T + 2], F32)
    fm_s = [dsc(f"fm_s{e}", [8, 128, NB, 4, 128], BF16) for e in range(2)]
    tm_s = [dsc(f"tm_s{e}", [NB, 128, 2, 1024], BF16) for e in range(2)]
    vtm_s = dsc("vtm_s", [NB, 128, 1024], BF16)
    gC_s = [dsc(f"gC_s{e}", [128, 8, NB], F32) for e in range(2)]
    g_s = dsc("g_s", [1024, T], F32)
    bt_s = dsc("bt_s", [1024, T], F32)
    y_s = [dsc(f"y_s{e}", [1024, T], F32) for e in range(2)]
    attT_s = dsc("attT_s", [1024, T], F32)

    import contextlib
    es = contextlib.ExitStack()

    def sb(name, shape, dt):
        return es.enter_context(nc.sbuf_tensor(name, list(shape), dt))

    with es:
        esems = {e: es.enter_context(nc.semaphore("s_" + e)) for e in Sched.ENG}
        dsems = [es.enter_context(nc.semaphore(f"d{j}")) for j in range(48)]
        S = Sched(esems, dsems)
        ps = [es.enter_context(nc.psum_tensor(f"ps{i}", [128, 512], F32)) for i in range(8)]
        PK = [f"ps{i}" for i in range(8)]
        ARN = 48000
        arena = sb("arena", [128, ARN], F32)
        apos = [0]

        def areset():
            apos[0] = 0

        def take(shape, dt=F32):
            n = int(np.prod(shape[1:]))
            cols = n if dt == F32 else (n + 1) // 2
            cols = (cols + 7) // 8 * 8
            a = apos[0]
            apos[0] += cols
            assert apos[0] <= ARN, ("arena overflow", apos[0])
            v = arena[: shape[0], a:a + cols]
            if dt != F32:
                v = v.bitcast(dt)
            v = v[:, 0:n]
            if len(shape) == 3:
                v = v.rearrange("p (a b) -> p a b", a=shape[1])
            elif len(shape) == 4:
                v = v.rearrange("p (a b c) -> p a b c", a=shape[1], b=shape[2])
            return v

        def bc(ap2, reps):
            a = ap2.ap
            assert len(a) == 2
            return bass.AP(ap2.tensor, ap2.offset, [list(a[0]), [0, reps], list(a[1])])

        def bcl(ap2, reps):
            a = ap2.ap
            assert len(a) == 2
            return bass.AP(ap2.tensor, ap2.offset, [list(a[0]), list(a[1]), [0, reps]])

        def mm(out, lhsT, rhs, start, stop, r, w):
            S.op("pe", lambda e: e.matmul(out, lhsT=lhsT, rhs=rhs, start=start, stop=stop), r, w)

        def tr(out, in_, ident, r, w):
            S.op("pe", lambda e: e.transpose(out, in_, ident), r, w)

        def act(out, in_, func, r, w, bias=0.0, scale=1.0):
            S.op("act", lambda e: e.activation(out, in_, func, bias=bias, scale=scale), r, w)

        def tt(eng, out, in0, in1, op, r, w):
            S.op(eng, lambda e: e.tensor_tensor(out, in0, in1, op), r, w)

        def ts(eng, out, in0, s1, op0, r, w, s2=None, op1=None):
            if op1 is None:
                S.op(eng, lambda e: e.tensor_scalar(out, in0, s1, None, op0), r, w)
            else:
                S.op(eng, lambda e: e.tensor_scalar(out, in0, s1, s2, op0, op1), r, w)

        def stt(out, in0, scalar, in1, op0, op1, r, w):
            S.op("dve", lambda e: e.scalar_tensor_tensor(out, in0, scalar, in1, op0, op1), r, w)

        def cp(eng, out, in_, r, w):
            if eng == "act":
                S.op("act", lambda e: e.copy(out, in_), r, w)
            else:
                S.op(eng, lambda e: e.tensor_copy(out, in_), r, w)

        def recip(out, in_, r, w):
            S.op("dve", lambda e: e.reciprocal(out, in_), r, w)

        def dma(out, in_, r, w, q="sp", slow=False):
            if slow:
                S.op(q, lambda e: e.dma_start(out=out, in_=in_, allow_slow_non_contiguous=True), r, w, dma=True)
            else:
                S.op(q, lambda e: e.dma_start(out=out, in_=in_), r, w, dma=True)

        def dst(out, in_, r, w, slow=False):
            dma(out, in_, r, w, q="pool", slow=slow)

        cpi = [0]

        cpr = [2]

        def cpa(out, in_, r, w):
            cpi[0] = (cpi[0] + 1) % cpr[0]
            cp("dve" if cpi[0] == 0 else "act", out, in_, r, w)

        psi = [0]

        def nps():
            psi[0] = (psi[0] + 1) % 8
            return psi[0]

        C = ("consts",)
        ident_f = sb("ident_f", [128, 128], F32)
        ident_b = sb("ident_b", [128, 128], BF16)
        ones_b = sb("ones_b", [128, 128], BF16)
        blk_f = sb("blk_f", [128, 128], F32)
        msk4 = sb("msk4", [128, 2, 512], F32)
        mskl = sb("mskl", [128, 2, 128], F32)
        rst = sb("rst", [128, 512], F32)
        c16 = sb("c16", [128, 3, 16], F32)
        c8 = sb("c8", [128, 10, 8], F32)
        cmu = sb("cmu", [128, 3, 28], F32)
        omka = sb("omka", [128, 8], F32)
        linkt = sb("linkt", [128, 1], F32)
        epsc = sb("epsc", [128, 2], F32)
        w2b = sb("w2b", [128, 1024], BF16)
        a2b = sb("a2b", [128, 1024], BF16)
        g2b0 = sb("g2b0", [128, 1024], BF16)
        g2b1 = sb("g2b1", [32, 1024], BF16)
        zcol = sb("zcol", [128, 28, 2], F32)

        areset()
        cst = take([128, 1536])
        ldf = take([128, 1024])
        dma(cst, cst_in[:, :], (), ("cst",))
        cp("dve", ident_f[:], cst[:, 0:128], ("cst",), C)
        cp("dve", ident_b[:], cst[:, 0:128], ("cst",), C)
        S.op("dve", lambda e: e.memset(ones_b[:], 1.0), (), C)
        for e_ in range(2):
            cp("dve", msk4[:, e_, 0:256], cst[:, 128 + e_ * 256:128 + (e_ + 1) * 256], ("cst",), C)
            cp("dve", msk4[:, e_, 256:512], cst[:, 128 + e_ * 256:128 + (e_ + 1) * 256], ("cst",), C)
        cp("dve", mskl[:].rearrange("p a c -> p (a c)"), cst[:, 640:896], ("cst",), C)
        cp("dve", blk_f[:], cst[:, 896:1024], ("cst",), C)
        cp("dve", rst[:], cst[:, 1024:1536], ("cst",), C)
        S.op("dve", lambda e: e.memset(zcol[:], 0.0), (), C)
        S.op("dve", lambda e: e.memset(epsc[:, 0:1], 1e-6), (), C)
        S.op("dve", lambda e: e.memset(epsc[:, 1:2], 64e-5), (), C)
        dma(c16[:], pk16[:, :, :], (), C)
        dma(c8[:], pk8[:, :, :], (), C)
        dma(cmu[:, 0:2, :], mu_in[:, :, :], (), C)
        dma(linkt[:], link_in[:, :], (), C)
        tt("dve", cmu[:, 2, :], cmu[:, 0, :], cmu[:, 1, :], ALU.add, C, C)
        ts("dve", cmu[:, 2, :], cmu[:, 2, :], -1.0, ALU.mult, C, C, 1.0, ALU.add)
        ts("dve", omka[:], c8[:, 2, :], -1.0, ALU.mult, C, C, 1.0, ALU.add)
        for src, dstt in ((w2_in, w2b), (a2_in, a2b), (g2_in, g2b0)):
            dma(ldf, src[0:128, :], (), ("ldf",))
            cp("dve", dstt[:, :], ldf, ("ldf",), C)
        dma(ldf[:32, :], g2_in[128:160, :], (), ("ldf",))
        cp("dve", g2b1[:, :], ldf[:32, :], ("ldf",), C)
        dst(raw_s[:, 0:1].rearrange("(c p) o -> p c o", p=128), zcol[:, :, 0:1], C, ("raw_s",), slow=True)
        dst(raw_s[:, T + 1:T + 2].rearrange("(c p) o -> p c o", p=128), zcol[:, :, 1:2], C, ("raw_s",), slow=True)
        S.barrier()

        areset()
        wl = [take([128, 3280]) for i in range(2)]
        wc = [take([128, 3280], BF16) for i in range(2)]
        wi = 0
        for src, dstt, rows, cols in ((w_in, wbf_in, D, PCOLS), (w_out, wbf_out, D, D), (w_gate, wbf_g, D, DFF),
                                      (w_up, wbf_u, D, DFF), (w_down, wbf_d, DFF, D)):
            cw = 3280 if cols == PCOLS else (2816 if cols == DFF else 2048)
            for rb in range(rows // 128):
                for cb in range(cols // cw):
                    i = wi % 2
                    wi += 1
                    dma(wl[i][:, :cw], src[rb * 128:(rb + 1) * 128, cb * cw:(cb + 1) * cw], (), (f"wl{i}",))
                    cpa(wc[i][:, :cw], wl[i][:, :cw], (f"wl{i}",), (f"wc{i}",))
                    dst(dstt[rb * 128:(rb + 1) * 128, cb * cw:(cb + 1) * cw], wc[i][:, :cw], (f"wc{i}",), (dstt.name,))
        S.barrier()

        G = {}

        def rms_fm(src, srck, gi, dst_t, dstk):
            sqr, rstd = G["sqr"], G["rstd"]
            b = nps()
            for k in range(KC):
                i = k % 4
                act(sqr[:, i, :], src[:, k, :], AF.Square, (srck,), (f"sqr{i}",))
                mm(ps[b][:, :], ones_b[:, :], sqr[:, i, :], k == 0, k == KC - 1, (f"sqr{i}",) + C, (PK[b],))
            act(rstd, ps[b][:, :], AF.Sqrt, (PK[b],) + C, ("rstd",), bias=epsc[:, 0:1], scale=1.0 / D)
            recip(rstd, rstd, ("rstd",), ("rstd",))
            for k in range(KC):
                stt(dst_t[:, k, :], src[:, k, :], c16[:, gi, k:k + 1], rstd, ALU.mult, ALU.mult,
                    (srck, "rstd") + C, (dstk,))

        wbi = [0]

        def gemm_fm(wsrc, ncols, rhs_t, rhsk, nk, evac, col0=0, gc=512):
            wb = G["wb"]
            c = 0
            while c * 128 < ncols:
                gcols = min(gc, ncols - c * 128)
                i = wbi[0] % 2
                wbi[0] += 1
                wv = wb[i][:, 0:nk * gcols].rearrange("p (k c) -> p k c", k=nk)
                dma(wv, wsrc[:, col0 + c * 128:col0 + c * 128 + gcols].rearrange("(k p) c -> p k c", p=128),
                    (wsrc.name,), (f"wb{i}",))
                for cc in range((gcols + 127) // 128):
                    m = min(128, gcols - cc * 128)
                    b = nps()
                    for k in range(nk):
                        mm(ps[b][:m, :], wv[:, k, cc * 128:cc * 128 + m], rhs_t[:, k, :], k == 0, k == nk - 1,
                           (f"wb{i}", rhsk), (PK[b],))
                    evac(c + cc, m, b)
                c += (gcols + 127) // 128

        if last >= 1:
            areset()
            xin_t = take([128, 2, 2048])
            xTa = [take([128, KC, 512]) for i in range(2)]
            hnTa = [take([128, KC, 512], BF16) for i in range(2)]
            fsqa = take([128, 4, 512], BF16)
            rstda = take([128, 512])
            G["wb"] = [take([128, 11264], BF16) for i in range(2)]
            stg = [take([128, 512]) for i in range(4)]
            stgb = [take([128, 512], BF16) for i in range(4)]
            sti = [0]
            bmode = [None]
            fba = [0]

            def nps_a():
                if bmode[0] == "front":
                    fba[0] ^= 1
                    return 6 + fba[0]
                psi[0] = (psi[0] + 1) % 6
                return psi[0]

            def front_a(t):
                t0 = t * 512
                xT, hnT = xTa[t % 2], hnTa[t % 2]
                xk, hk = f"xT{t % 2}", f"hnT{t % 2}"
                for bb in range(2):
                    dma(xin_t, x_in[t0 + bb * 256:t0 + bb * 256 + 256, :].rearrange("(b p) d -> p b d", p=128),
                        (), ("xin",))
                    for k in range(KC):
                        b = nps_a()
                        for b2 in range(2):
                            tr(ps[b][:, b2 * 128:(b2 + 1) * 128], xin_t[:, b2, k * 128:(k + 1) * 128], ident_f[:],
                               ("xin",) + C, (PK[b],))
                        cpa(xT[:, k, bb * 256:bb * 256 + 256], ps[b][:, 0:256], (PK[b],), (xk,))
                        if k % 2 == 1:
                            yield
                dst(xT_s[:, :, t0:t0 + 512].rearrange("k p t -> p k t"), xT, (xk,), ("xT_s",))
                b = nps_a()
                for k in range(KC):
                    i = k % 4
                    act(fsqa[:, i, :], xT[:, k, :], AF.Square, (xk,), (f"fsqa{i}",))
                    mm(ps[b][:, :], ones_b[:, :], fsqa[:, i, :], k == 0, k == KC - 1, (f"fsqa{i}",) + C, (PK[b],))
                act(rstda, ps[b][:, :], AF.Sqrt, (PK[b],) + C, ("rstda",), bias=epsc[:, 0:1], scale=1.0 / D)
                recip(rstda, rstda, ("rstda",), ("rstda",))
                yield
                for k in range(KC):
                    stt(hnT[:, k, :], xT[:, k, :], c16[:, 0, k:k + 1], rstda, ALU.mult, ALU.mult, (xk, "rstda") + C, (hk,))
                    if k % 2 == 1:
                        yield

            def main_a(t):
                t0 = t * 512
                hnT, hk = hnTa[t % 2], f"hnT{t % 2}"

                def evac_a(c, b, m=128):
                    i = sti[0] % 4
                    sti[0] += 1
                    if c < 8:
                        act(stgb[i], ps[b][:, :], AF.Copy, (PK[b],), (f"stgb{i}",), scale=0.125)
                        dst(qT_s[c * 128:(c + 1) * 128, t0:t0 + 512], stgb[i], (f"stgb{i}",), ("qT_s",))
                    elif c < 16:
                        cpa(stgb[i], ps[b][:, :], (PK[b],), (f"stgb{i}",))
                        dst(kT_s[(c - 8) * 128:(c - 7) * 128, t0:t0 + 512], stgb[i], (f"stgb{i}",), ("kT_s",))
                    else:
                        cr = c - 24
                        cpa(stg[i][:m, :], ps[b][:m, :], (PK[b],), (f"stg{i}",))
                        dst(raw_s[cr * 128:cr * 128 + m, 1 + t0:1 + t0 + 512], stg[i][:m, :], (f"stg{i}",), ("raw_s",))

                def gemm_cols(col0, ncols, cbase):
                    c = 0
                    while c * 128 < ncols:
                        gcols = min(512, ncols - c * 128)
                        i = wbi[0] % 2
                        wbi[0] += 1
                        wv = G["wb"][i][:, 0:KC * gcols].rearrange("p (k c) -> p k c", k=KC)
                        dma(wv, wbf_in[:, col0 + c * 128:col0 + c * 128 + gcols].rearrange("(k p) c -> p k c", p=128),
                            ("wbf_in",), (f"wb{i}",))
                        for cc in range((gcols + 127) // 128):
                            m = min(128, gcols - cc * 128)
                            b = nps_a()
                            for k in range(KC):
                                mm(ps[b][:m, :], wv[:, k, cc * 128:cc * 128 + m], hnT[:, k, :], k == 0, k == KC - 1,
                                   (f"wb{i}", hk), (PK[b],))
                            evac_a(cbase + c + cc, b, m)
                            yield
                        c += (gcols + 127) // 128

                yield from gemm_cols(0, 2048, 0)
                for g in range(2):
                    i = wbi[0] % 2
                    wbi[0] += 1
                    wv = G["wb"][i][:, 0:KC * 512].rearrange("p (k c) -> p k c", k=KC)
                    dma(wv, wbf_in[:, 2048 + g * 512:2048 + (g + 1) * 512].rearrange("(k p) c -> p k c", p=128),
                        ("wbf_in",), (f"wb{i}",))
                    for b4 in range(4):
                        b = nps_a()
                        for k in range(KC):
                            mm(ps[b][:, :], hnT[:, k, b4 * 128:(b4 + 1) * 128], wv[:, k, :], k == 0, k == KC - 1,
                               (f"wb{i}", hk), (PK[b],))
                        j = sti[0] % 4
                        sti[0] += 1
                        cpa(stgb[j], ps[b][:, :], (PK[b],), (f"stgb{j}",))
                        dst(v_s[t0 + b4 * 128:t0 + (b4 + 1) * 128, g * 512:(g + 1) * 512], stgb[j],
                            (f"stgb{j}",), ("v_s",))
                        yield
                yield from gemm_cols(3072, PCOLS - 3072, 24)

            def run_fa(g, n):
                bmode[0] = "front"
                done = False
                for _ in range(n):
                    try:
                        next(g)
                    except StopIteration:
                        done = True
                        break
                bmode[0] = None
                return done

            run_fa(front_a(0), 10 ** 9)
            for t in range(NT):
                fg = front_a(t + 1) if t + 1 < NT else None
                for _ in main_a(t):
                    if fg is not None and run_fa(fg, 1):
                        fg = None
                if fg is not None:
                    run_fa(fg, 10 ** 9)
            S.barrier()

        if last >= 2:
            areset()
            NW = 4
            win = [take([128, 514]) for i in range(NW)]
            t1b = [take([128, 512]) for i in range(2)]
            t2b = [take([128, 512]) for i in range(2)]
            shl = take([128, 512])
            wdS = take([128, 512], BF16)
            adS = take([128, 512], BF16)
            gdS0 = take([128, 512], BF16)
            gdS1 = take([32, 512], BF16)

            def mkset(names):
                return {nm: take([128, 512]) for nm in names}
            SA, SB, FMST = [], [], []
            for q in range(2):
                d = mkset(("rS", "kS", "vS", "sg0", "sg1", "av0", "av1", "kkr", "sqk", "rsq", "kk", "tmpk0", "tmpk1",
                           "kd0", "kd1"))
                d["vb"] = take([128, 512], BF16)
                d["prr"], d["ksum"], d["gst"], d["bterm"] = d["sqk"], d["rsq"], d["tmpk0"], d["tmpk1"]
                SA.append(d)
                d2 = mkset(("lw", "cc", "cx", "dd", "di", "eIn", "eNeg", "eEx", "eHat", "kb"))
                d2["KH"], d2["BH"] = take([128, 512], BF16), take([128, 512], BF16)
                SB.append(d2)
                FMST.append([take([128, 4, 4, 128], BF16) for e in range(2)])
            AL = {"prr": "sqk", "ksum": "rsq", "gst": "tmpk0", "bterm": "tmpk1"}
            tmst = [take([128, 4, 2, 1024], BF16) for e in range(2)]
            vtmst = take([128, 4, 1024], BF16)
            gcst = take([128, 2, 8, 4])
            wi_ = [0]

            def loadwin(cr, t, m=128):
                t0 = t * 512
                i = wi_[0] % NW
                wi_[0] += 1
                w_ = win[i]
                dma(w_[:m, :], raw_s[cr * 128:cr * 128 + m, t0:t0 + 514], ("raw_s",), (f"win{i}",))
                if t0 + 512 == HT:
                    ts("dve", w_[:m, 513:514], w_[:m, 513:514], linkt[:m, 0:1], ALU.mult, (f"win{i}",) + C, (f"win{i}",))
                if t0 == HT:
                    ts("dve", w_[:m, 0:1], w_[:m, 0:1], linkt[:m, 0:1], ALU.mult, (f"win{i}",) + C, (f"win{i}",))
                return w_, f"win{i}"

            shi = [0]

            def shift(cr, t, out, outk, m=128):
                w_, wk = loadwin(cr, t, m)
                i = shi[0] % 2
                shi[0] += 1
                act(t1b[i][:m, :], w_[:m, 1:513], AF.Identity, (wk,) + C, (f"t1b{i}",), scale=cmu[:m, 2, cr:cr + 1])
                stt(t2b[i][:m, :], w_[:m, 0:512], cmu[:m, 0, cr:cr + 1], t1b[i][:m, :], ALU.mult, ALU.add,
                    (wk, f"t1b{i}") + C, (f"t2b{i}",))
                stt(out[:m, :], w_[:m, 2:514], cmu[:m, 1, cr:cr + 1], t2b[i][:m, :], ALU.mult, ALU.add,
                    (wk, f"t2b{i}") + C, (outk,))

            def v3(a):
                return a.rearrange("p (c t) -> p c t", c=4)

            def jgen(j, t, q):
                t0 = t * 512
                jc = slice(j * 128, (j + 1) * 128)
                A_, B_ = SA[q], SB[q]
                K_ = lambda nm: f"{AL.get(nm, nm)}_{q}"
                rS, kS, vS, kk = A_["rS"], A_["kS"], A_["vS"], A_["kk"]
                shift(j, t, rS, K_("rS"))
                yield
                shift(8 + j, t, kS, K_("kS"))
                yield
                shift(16 + j, t, vS, K_("vS"))
                yield
                for e in range(2):
                    er = slice(64 * e, 64 * e + 64)
                    b = nps()
                    mm(ps[b][:, :], w2b[er, jc], wdS[er, :], True, True, ("wdS",) + C, (PK[b],))
                    act(A_[f"sg{e}"], ps[b][:, :], AF.Sigmoid, (PK[b],) + C, (K_(f"sg{e}"),), bias=c8[:, 6 + e, j:j + 1])
                    b = nps()
                    mm(ps[b][:, :], a2b[er, jc], adS[er, :], True, True, ("adS",) + C, (PK[b],))
                    act(A_[f"av{e}"], ps[b][:, :], AF.Sigmoid, (PK[b],) + C, (K_(f"av{e}"),), bias=c8[:, 8 + e, j:j + 1])
                    yield
                b = nps()
                mm(ps[b][:, :], g2b0[:, jc], gdS0, True, False, ("gdS0",) + C, (PK[b],))
                mm(ps[b][:, :], g2b1[:, jc], gdS1, False, True, ("gdS1",) + C, (PK[b],))
                cp("act", A_["gst"], ps[b][:, :], (PK[b],), (K_("gst"),))
                dst(g_s[jc, t0:t0 + 512], A_["gst"], (K_("gst"),), ("g_s",))
                ts("dve", A_["kkr"], kS, c8[:, 1, j:j + 1], ALU.mult, (K_("kS"),) + C, (K_("kkr"),))
                act(A_["sqk"], A_["kkr"], AF.Square, (K_("kkr"),), (K_("sqk"),))
                yield
                b = nps()
                mm(ps[b][:, :], blk_f[:, :], A_["sqk"], True, True, (K_("sqk"),) + C, (PK[b],))
                ts("dve", A_["rsq"], ps[b][:, :], 1e-24, ALU.max, (PK[b],), (K_("rsq"),))
                act(A_["rsq"], A_["rsq"], AF.Sqrt, (K_("rsq"),), (K_("rsq"),))
                yield
                recip(A_["rsq"], A_["rsq"], (K_("rsq"),), (K_("rsq"),))
                yield
                tt("dve", kk, A_["kkr"], A_["rsq"], ALU.mult, (K_("kkr"), K_("rsq")), (K_("kk"),))
                for e in range(2):
                    ts("dve", A_[f"tmpk{e}"], A_[f"av{e}"], c8[:, 2, j:j + 1], ALU.mult, (K_(f"av{e}"),) + C,
                       (K_(f"tmpk{e}"),), omka[:, j:j + 1], ALU.add)
                    yield
                    tt("dve", A_[f"kd{e}"], A_[f"tmpk{e}"], kS, ALU.mult, (K_(f"tmpk{e}"), K_("kS")), (K_(f"kd{e}"),))
                yield
                tt("dve", A_["ksum"], A_["kd0"], A_["kd1"], ALU.add, (K_("kd0"), K_("kd1")), (K_("ksum"),))
                yield
                stt(A_["prr"], rS, c8[:, 3, j:j + 1], A_["ksum"], ALU.mult, ALU.mult, (K_("rS"), K_("ksum")) + C,
                    (K_("prr"),))
                yield
                b = nps()
                mm(ps[b][:, :], blk_f[:, :], A_["prr"], True, True, (K_("prr"),) + C, (PK[b],))
                tt("dve", A_["bterm"], ps[b][:, :], vS, ALU.mult, (PK[b], K_("vS")), (K_("bterm"),))
                dst(bt_s[jc, t0:t0 + 512], A_["bterm"], (K_("bterm"),), ("bt_s",))
                cp("act", A_["vb"], vS, (K_("vS"),), (K_("vb"),))
                yield
                lw, cc_, cx, dd, di = B_["lw"], B_["cc"], B_["cx"], B_["dd"], B_["di"]
                eIn, eNeg, eEx, eHat, kb, KH, BH = (B_[n] for n in ("eIn", "eNeg", "eEx", "eHat", "kb", "KH", "BH"))
                Q = lambda nm: f"{nm}_{q}"
                for e in range(2):
                    sg_, av_, kd_ = A_[f"sg{e}"], A_[f"av{e}"], A_[f"kd{e}"]
                    sgk, avk, kdk = K_(f"sg{e}"), K_(f"av{e}"), K_(f"kd{e}")
                    ts("dve", lw, sg_, -C0, ALU.mult, (sgk,), (Q("lw"),))
                    yield
                    S.op("dve", lambda en: en.tensor_tensor_scan(cc_, rst[:, :], lw, 0.0, ALU.mult, ALU.add),
                         (Q("lw"),) + C, (Q("cc"),))
                    yield
                    tt("dve", cx, cc_, lw, ALU.subtract, (Q("cc"), Q("lw")), (Q("cx"),))
                    tt("dve", v3(dd), bcl(v3(cc_)[:, :, 127], 128), v3(cc_), ALU.subtract, (Q("cc"),), (Q("dd"),))
                    yield
                    if e == 0:
                        Ein, Eex, Ehat = cc_, cx, dd
                        ek = (Q("cc"), Q("cx"), Q("dd"))
                    else:
                        tt("dve", di, dd, lw, ALU.add, (Q("dd"), Q("lw")), (Q("di"),))
                        Ein, Eex, Ehat = di, dd, cx
                        ek = (Q("di"), Q("dd"), Q("cx"))
                        yield
                    act(eIn, Ein, AF.Exp, (ek[0],), (Q("eIn"),))
                    act(eNeg, Ein, AF.Exp, (ek[0],), (Q("eNeg"),), scale=-1.0)
                    yield
                    act(eEx, Eex, AF.Exp, (ek[1],), (Q("eEx"),))
                    act(eHat, Ehat, AF.Exp, (ek[2],), (Q("eHat"),))
                    act(gcst[:, e, j, :], v3(cc_)[:, :, 127], AF.Exp, (Q("cc"),), ("gcst",))
                    yield
                    fs = FMST[q][e]
                    fk = f"fmst{q}{e}"
                    tt("dve", kb, kk, av_, ALU.mult, (K_("kk"), avk), (Q("kb"),))
                    tt("dve", fs[:, :, 1, :], v3(rS), v3(eIn), ALU.mult, (K_("rS"), Q("eIn")), (fk,))
                    yield
                    tt("dve", fs[:, :, 2, :], v3(kd_), v3(eNeg), ALU.mult, (kdk, Q("eNeg")), (fk,))
                    yield
                    stt(fs[:, :, 0, :], v3(kk), -1.0, v3(eEx), ALU.mult, ALU.mult, (K_("kk"), Q("eEx")), (fk,))
                    tt("dve", fs[:, :, 3, :], v3(kb), v3(eNeg), ALU.mult, (Q("kb"), Q("eNeg")), (fk,))
                    yield
                    dst(fm_s[e][j, :, t * 4:(t + 1) * 4, :, :], fs, (fk,), (f"fm_s{e}",))
                    tt("dve", KH, kd_, eHat, ALU.mult, (kdk, Q("eHat")), (Q("KH"),))
                    tt("dve", BH, kb, eHat, ALU.mult, (Q("kb"), Q("eHat")), (Q("BH"),))
                    yield
                    b = nps()
                    pb = ps[b][:, :].bitcast(BF16).rearrange("p (w b c) -> p w b c", w=2, b=4)
                    for b4 in range(4):
                        tr(pb[:, 0, b4, :], KH[:, b4 * 128:(b4 + 1) * 128], ident_b[:], (Q("KH"),) + C, (PK[b],))
                    for b4 in range(4):
                        tr(pb[:, 1, b4, :], BH[:, b4 * 128:(b4 + 1) * 128], ident_b[:], (Q("BH"),) + C, (PK[b],))
                    for w_ in range(2):
                        cp("act", tmst[e][:, :, w_, jc], pb[:, w_, :, :], (PK[b],), (f"tmst{e}",))
                    yield
                b = nps()
                pb = ps[b][:, 0:256].bitcast(BF16).rearrange("p (b c) -> p b c", b=4)
                for b4 in range(4):
                    tr(pb[:, b4, :], A_["vb"][:, b4 * 128:(b4 + 1) * 128], ident_b[:], (K_("vb"),) + C, (PK[b],))
                cp("act", vtmst[:, :, jc], pb, (PK[b],), ("vtmst",))
                yield

            for t in range(NT):
                t0 = t * 512
                shift(24, t, shl, "shl")
                act(wdS, shl, AF.Tanh, ("shl",), ("wdS",))
                shift(25, t, shl, "shl")
                cp("dve", adS, shl, ("shl",), ("adS",))
                shift(26, t, shl, "shl")
                act(gdS0, shl, AF.Sigmoid, ("shl",), ("gdS0",))
                shift(27, t, shl, "shl", m=32)
                act(gdS1, shl[:32, :], AF.Sigmoid, ("shl",), ("gdS1",))
                for jp in range(4):
                    gens = [jgen(2 * jp, t, 0), jgen(2 * jp + 1, t, 1)]
                    alive = [True, True]
                    lead = 2
                    for _ in range(lead):
                        try:
                            next(gens[0])
                        except StopIteration:
                            alive[0] = False
                            break
                    while alive[0] or alive[1]:
                        for gi in (1, 0):
                            if alive[gi]:
                                try:
                                    next(gens[gi])
                                except StopIteration:
                                    alive[gi] = False
                for e in range(2):
                    dst(tm_s[e][t * 4:(t + 1) * 4, :, :, :].rearrange("b p w c -> p b w c"), tmst[e], (f"tmst{e}",),
                        (f"tm_s{e}",))
                    dst(gC_s[e][:, :, t * 4:(t + 1) * 4], gcst[:, e, :, :], ("gcst",), (f"gC_s{e}",))
                dst(vtm_s[t * 4:(t + 1) * 4, :, :].rearrange("b p c -> p b c"), vtmst, ("vtmst",), ("vtm_s",))
            S.barrier()

        if last >= 3:
            areset()
            QT = [take([128, 2, 8, 64], BF16) for i in range(2)]
            for i in range(2):
                S.op("dve", lambda en, i=i: en.memset(QT[i], 0.0), (), (f"QT{i}",))
            KTb = [take([128, 8, 1024], BF16) for i in range(2)]
            Vb = [take([128, 8, 1024], BF16) for i in range(2)]
            tabI = take([128, 16, 4, 64], BF16)
            tabX = take([128, 16, 8, 64], BF16)
            tabF = take([128, 16, 8, 64])
            PT = [take([128, 16, 8, 64], BF16) for i in range(2)]
            rec = take([128, 8, 64])
            ost = [take([128, 8, 64]) for i in range(2)]
            dma(tabF[:, :, 0:4, :], tab_in[0], (), ("tabF",))
            cp("dve", tabI, tabF[:, :, 0:4, :], ("tabF",), ("tabI",))
            for r in range(NR):
                i = r % 2
                if RS - 3 <= r <= RS + 3:
                    kr0, nk = RS - 8, 16
                    dma(tabF, wtab_in[r - (RS - 3)], (), ("tabF",))
                    cp("dve", tabX, tabF, ("tabF",), ("tabX",))
                    tab, tabk = tabX, "tabX"
                else:
                    base = 0 if r < RS else RS
                    rl = r - base
                    kr0, nk = int(np.clip(rl - 4, 0, RS - 8)) + base, 8
                    ty = 0
                    if rl < 4:
                        ty = 1 + rl
                    elif rl > RS - 4:
                        ty = 5 + (2 - (RS - 1 - rl))
                    if ty == 0:
                        tab, tabk = tabI, "tabI"
                    else:
                        dma(tabF[:, :, 0:4, :], tab_in[ty], (), ("tabF",))
                        cp("dve", tabX[:, :, 0:4, :], tabF[:, :, 0:4, :], ("tabF",), ("tabX",))
                        tab, tabk = tabX, "tabX"
                nkc = nk // 2
                qsrc = qT_s[:, r * 64:(r + 1) * 64].rearrange("(j p) q -> p j q", p=128)
                dma(QT[i][0:64, 0, :, :], qsrc[0:64], ("qT_s",), (f"QT{i}",))
                dma(QT[i][64:128, 1, :, :], qsrc[64:128], ("qT_s",), (f"QT{i}",))
                dma(KTb[i][:, :, 0:nk * 64], kT_s[:, kr0 * 64:(kr0 + nk) * 64].rearrange("(j p) q -> p j q", p=128),
                    ("kT_s",), (f"KT{i}",))
                dma(Vb[i][:, 0:nkc, :], v_s[kr0 * 64:(kr0 + nk) * 64, :].rearrange("(k p) c -> p k c", p=128),
                    ("v_s",), (f"V{i}",))
                for h in range(16):
                    j, p = h // 2, h % 2
                    pr = slice(64 * p, 64 * p + 64)
                    if nkc == 4:
                        if h % 2 == 0:
                            b = nps()
                        off = (h % 2) * 256
                    else:
                        b = nps()
                        off = 0
                    for kc in range(nkc):
                        o_ = ps[b][:, off + kc * 64:off + (kc + 1) * 64]
                        mm(o_, KTb[i][:, j, kc * 128:(kc + 1) * 128], QT[i][:, p, j, :], True, False,
                           (f"KT{i}", f"QT{i}"), (PK[b],))
                        mm(o_, ident_b[:, :], tab[:, h, kc, :], False, True, (tabk,) + C, (PK[b],))
                    if nkc == 8 or h % 2 == 1:
                        if nkc == 4:
                            act(PT[i][:, h - 1:h + 1, 0:4, :], ps[b][:, :].rearrange("p (h k c) -> p h k c", h=2, k=4),
                                AF.Exp, (PK[b],), (f"PT{i}",))
                        else:
                            act(PT[i][:, h, :, :], ps[b][:, :].rearrange("p (k c) -> p k c", k=8), AF.Exp, (PK[b],),
                                (f"PT{i}",))
                ba, bb_ = nps(), nps()
                for h in range(16):
                    j, p = h // 2, h % 2
                    pr = slice(64 * p, 64 * p + 64)
                    b = ba if j < 4 else bb_
                    o0 = (j % 4) * 128
                    for kc in range(nkc):
                        mm(ps[b][pr, o0:o0 + 64], Vb[i][:, kc, h * 64:(h + 1) * 64], PT[i][:, h, kc, :], kc == 0,
                           kc == nkc - 1, (f"V{i}", f"PT{i}"), (PK[b],))
                    for kc in range(nkc):
                        mm(ps[b][pr, o0 + 64:o0 + 128], ones_b[:, 0:64], PT[i][:, h, kc, :], kc == 0, kc == nkc - 1,
                           (f"PT{i}",) + C, (PK[b],))
                for half, b in ((0, ba), (1, bb_)):
                    pv = ps[b][:, :].rearrange("p (j x c) -> p j x c", j=4, x=2)
                    recip(rec[:, half * 4:(half + 1) * 4, :], pv[:, :, 1, :], (PK[b],), ("rec",))
                    tt("dve", ost[i][:, half * 4:(half + 1) * 4, :], pv[:, :, 0, :], rec[:, half * 4:(half + 1) * 4, :],
                       ALU.mult, (PK[b], "rec"), (f"ost{i}",))
                dst(attT_s[:, r * 64:(r + 1) * 64].rearrange("(j p) q -> p j q", p=128), ost[i], (f"ost{i}",), ("attT_s",))
            S.barrier()

        if last >= 4:
            areset()
            FMt = [[take([128, 8, 4, 128], BF16) for i in range(2)] for e in range(2)]
            TMt = [[take([128, 2, 1024], BF16) for i in range(2)] for e in range(2)]
            Vt = [[take([128, 1024], BF16) for i in range(2)] for e in range(2)]
            gCt = [take([128, 8, NB]) for e in range(2)]
            Hs = [take([128, 8, 64]) for e in range(2)]
            Hb = [take([128, 8, 2, 64], BF16) for e in range(2)]
            ZRz = [take([128, 8, 2, 256], BF16) for e in range(2)]
            AT = [[take([128, 8, 512], BF16) for e in range(2)] for u in range(2)]
            PQ = [[[take([128, 8, 256], BF16) for i in range(2)] for e in range(2)] for u in range(2)]
            Xb = [[[take([128, 8, 128], BF16) for i in range(2)] for e in range(2)] for u in range(2)]
            Wsb = [take([128, 8, 64], BF16) for e in range(2)]
            Usb = [take([128, 8, 64], BF16) for e in range(2)]
            Yst = [[take([128, 4, 128]) for e in range(2)] for u in range(2)]
            for e in range(2):
                dma(gCt[e], gC_s[e][:, :, :], (f"gC_s{e}",), (f"gCt{e}",))
                S.op("dve", lambda en, e=e: en.memset(Hs[e], 0.0), (), (f"H{e}",))
                S.op("dve", lambda en, e=e: en.memset(Hb[e], 0.0), (), (f"Hb{e}",))

            def s1_parts(step, hf):
                si, up = step % 2, hf
                cis = (step, NB - 1 - step)
                parts = []

                def pA1():
                    if hf == 0:
                        for e in range(2):
                            ci = cis[e]
                            dma(FMt[e][si], fm_s[e][:, :, ci, :, :].rearrange("j p w t -> p j w t"), (f"fm_s{e}",),
                                (f"FM{e}{si}",))
                            dma(TMt[e][si], tm_s[e][ci], (f"tm_s{e}",), (f"TM{e}{si}",))
                            dma(Vt[e][si], vtm_s[ci], ("vtm_s",), (f"V{e}{si}",))
                    for e in range(2):
                        fm, fk, zk = FMt[e][si], f"FM{e}{si}", f"ZRz{e}"
                        if hf == 0:
                            zsrc = fm[:, :, 0:2, :].rearrange("p j w t -> p j (w t)")
                            ts("dve", ZRz[e][:, :, 0, :], zsrc, blk_f[:, 0:1], ALU.mult, (fk,) + C, (zk,))
                            act(ZRz[e][:, :, 1, :], zsrc, AF.Identity, (fk,) + C, (zk,), scale=blk_f[:, 64:65])
                        for hh in range(8):
                            h = hf * 8 + hh
                            j, p = h // 2, h % 2
                            b = nps()
                            mm(ps[b][:, 0:256], fm[:, j, 2, :], ZRz[e][:, j, p, :], True, True, (fk, zk), (PK[b],))
                            mm(ps[b][:, 256:512], fm[:, j, 3, :], ZRz[e][:, j, p, :], True, True, (fk, zk), (PK[b],))
                            tt("dve", AT[up][e][:, hh, :], ps[b][:, :], msk4[:, e, :], ALU.mult, (PK[b],) + C,
                               (f"AT{up}{e}",))
                parts.append(pA1)

                def pA2():
                    for e in range(2):
                        fm, fk, zk = FMt[e][si], f"FM{e}{si}", f"ZRz{e}"
                        for g in range(2):
                            b = nps()
                            for h4 in range(4):
                                hh = g * 4 + h4
                                h = hf * 8 + hh
                                j, p = h // 2, h % 2
                                mm(ps[b][:, h4 * 128:(h4 + 1) * 128], ZRz[e][:, j, p, 0:128], fm[:, j, 3, :], True, True,
                                   (fk, zk), (PK[b],))
                            tt("dve", PQ[up][e][0][:, g * 4:(g + 1) * 4, 128:256],
                               ps[b][:, :].rearrange("p (h c) -> p h c", h=4), bc(mskl[:, e, :], 4), ALU.mult,
                               (PK[b],) + C, (f"PQ{up}{e}0g{2 * g}", f"PQ{up}{e}0g{2 * g + 1}"))
                        cp("act", PQ[up][e][0][:, :, 0:128], AT[up][e][:, :, 256:384], (f"AT{up}{e}",),
                           tuple(f"PQ{up}{e}0g{g}" for g in range(4)))
                        tt("dve", Xb[up][e][0][:, :, :], AT[up][e][:, :, 256:384], bc(ident_b[:, :], 8), ALU.add,
                           (f"AT{up}{e}",) + C, (f"X{up}{e}0x0", f"X{up}{e}0x1"))
                parts.append(pA2)

                def mk_level(lv):
                    def pL():
                        cur, nxt = lv % 2, (lv + 1) % 2
                        for e in range(2):
                            for g in range(4):
                                pk_c, pk_n = f"PQ{up}{e}{cur}g{g}", f"PQ{up}{e}{nxt}g{g}"
                                b = nps()
                                for h2 in range(2):
                                    hh = g * 2 + h2
                                    Pc, Qc = PQ[up][e][cur][:, hh, 0:128], PQ[up][e][cur][:, hh, 128:256]
                                    if lv < 5:
                                        mm(ps[b][:, h2 * 256:h2 * 256 + 128], Qc, Pc, True, True, (pk_c,), (PK[b],))
                                    mm(ps[b][:, h2 * 256 + 128:h2 * 256 + 256], Pc, Qc, True, True, (pk_c,), (PK[b],))
                                if lv < 5:
                                    cpa(PQ[up][e][nxt][:, g * 2:g * 2 + 2, :],
                                        ps[b][:, :].rearrange("p (h c) -> p h c", h=2), (PK[b],), (pk_n,))
                                else:
                                    cpa(PQ[up][e][nxt][:, g * 2:g * 2 + 2, 128:256],
                                        ps[b][:, :].rearrange("p (h c) -> p h c", h=2)[:, :, 128:256], (PK[b],), (pk_n,))
                        for e in range(2):
                            for g in range(2):
                                xk_c, xk_n = f"X{up}{e}{cur}x{g}", f"X{up}{e}{nxt}x{g}"
                                b = nps()
                                for h4 in range(4):
                                    hh = g * 4 + h4
                                    o_ = ps[b][:, h4 * 128:(h4 + 1) * 128]
                                    mm(o_, ident_b[:, :], Xb[up][e][cur][:, hh, :], True, False, (xk_c,) + C, (PK[b],))
                                    mm(o_, PQ[up][e][nxt][:, hh, 128:256], Xb[up][e][cur][:, hh, :], False, True,
                                       (f"PQ{up}{e}{nxt}g{hh // 2}", xk_c), (PK[b],))
                                cpa(Xb[up][e][nxt][:, g * 4:(g + 1) * 4, :],
                                    ps[b][:, :].rearrange("p (h c) -> p h c", h=4), (PK[b],), (xk_n,))
                    return pL
                for lv in range(6):
                    parts.append(mk_level(lv))
                return parts

            def s2_parts(step, hf):
                si, up = step % 2, hf
                cis = (step, NB - 1 - step)
                XF = 0

                def ctx(e):
                    return (FMt[e][si], f"FM{e}{si}", TMt[e][si], f"TM{e}{si}", Vt[e][si], f"V{e}{si}",
                            (f"X{up}{e}{XF}x0", f"X{up}{e}{XF}x1"), f"AT{up}{e}")

                def pW():
                    for e in range(2):
                        fm, fk, tm, tk, vt, vk, xk, ak = ctx(e)
                        b = nps()
                        for hh in range(8):
                            h = hf * 8 + hh
                            j, p = h // 2, h % 2
                            o_ = ps[b][:, hh * 64:(hh + 1) * 64]
                            mm(o_, fm[:, j, 0, :], Hb[e][:, j, p, :], True, False, (fk, f"Hb{e}"), (PK[b],))
                            mm(o_, AT[up][e][:, hh, 0:128], vt[:, h * 64:(h + 1) * 64], False, True, (ak, vk), (PK[b],))
                        cpa(Wsb[e][:, :, :], ps[b][:, :].rearrange("p (h c) -> p h c", h=8), (PK[b],), (f"W{e}",))

                def pU():
                    for e in range(2):
                        fm, fk, tm, tk, vt, vk, xk, ak = ctx(e)
                        b = nps()
                        for hh in range(8):
                            mm(ps[b][:, hh * 64:(hh + 1) * 64], Xb[up][e][XF][:, hh, :], Wsb[e][:, hh, :], True, True,
                               xk + (f"W{e}",), (PK[b],))
                        cpa(Usb[e][:, :, :], ps[b][:, :].rearrange("p (h c) -> p h c", h=8), (PK[b],), (f"U{e}",))

                def pY():
                    for e in range(2):
                        ci = cis[e]
                        fm, fk, tm, tk, vt, vk, xk, ak = ctx(e)
                        b = nps()
                        for hh in range(8):
                            h = hf * 8 + hh
                            j, p = h // 2, h % 2
                            pr = slice(64 * p, 64 * p + 64)
                            o_ = ps[b][pr, (j % 4) * 128:(j % 4 + 1) * 128]
                            mm(o_, Hb[e][:, j, p, :], fm[:, j, 1, :], True, False, (fk, f"Hb{e}"), (PK[b],))
                            mm(o_, Usb[e][:, hh, :], AT[up][e][:, hh, 384:512], False, False, (f"U{e}", ak), (PK[b],))
                            mm(o_, vt[:, h * 64:(h + 1) * 64], AT[up][e][:, hh, 128:256], False, True, (vk, ak), (PK[b],))
                        cpa(Yst[up][e][:, :, :], ps[b][:, :].rearrange("p (j c) -> p j c", j=4), (PK[b],), (f"Yst{up}{e}",))
                        dst(y_s[e][hf * 512:(hf + 1) * 512, ci * 128:(ci + 1) * 128].rearrange("(j p) t -> p j t", p=128),
                            Yst[up][e], (f"Yst{up}{e}",), (f"y_s{e}",))

                def pH():
                    for e in range(2):
                        ci = cis[e]
                        fm, fk, tm, tk, vt, vk, xk, ak = ctx(e)
                        b = nps()
                        for hh in range(8):
                            h = hf * 8 + hh
                            j, p = h // 2, h % 2
                            pr = slice(64 * p, 64 * p + 64)
                            o_ = ps[b][pr, (j % 4) * 64:(j % 4 + 1) * 64]
                            mm(o_, tm[:, 1, h * 64:(h + 1) * 64], Usb[e][:, hh, :], True, False, (tk, f"U{e}"), (PK[b],))
                            mm(o_, tm[:, 0, h * 64:(h + 1) * 64], vt[:, h * 64:(h + 1) * 64], False, True, (tk, vk),
                               (PK[b],))
                        js = slice(hf * 4, (hf + 1) * 4)
                        tt("dve", Hs[e][:, js, :], Hs[e][:, js, :], bcl(gCt[e][:, js, ci], 64), ALU.mult,
                           (f"H{e}", f"gCt{e}"), (f"H{e}",))
                        tt("dve", Hs[e][:, js, :], Hs[e][:, js, :], ps[b][:, 0:256].rearrange("p (j c) -> p j c", j=4),
                           ALU.add, (f"H{e}", PK[b]), (f"H{e}",))
                        if (e == 0 and ci == NB // 2 - 1) or (e == 1 and ci == NB // 2):
                            ts("dve", Hs[e][:, js, :], Hs[e][:, js, :], linkt[:, 0:1], ALU.mult, (f"H{e}",) + C, (f"H{e}",))
                        ts("dve", Hb[e][:, js, 0, :], Hs[e][:, js, :], blk_f[:, 0:1], ALU.mult, (f"H{e}",) + C, (f"Hb{e}",))
                        act(Hb[e][:, js, 1, :], Hs[e][:, js, :], AF.Identity, (f"H{e}",) + C, (f"Hb{e}",),
                            scale=blk_f[:, 64:65])
                return [pW, pU, pY, pH]

            cpr[0] = 3
            units = [(st, hf) for st in range(NB) for hf in range(2)]
            prev2 = None
            for u in units + [None]:
                p1 = s1_parts(*u) if u is not None else []
                p2 = prev2 if prev2 is not None else []
                order = []
                i1 = i2 = 0
                while i1 < len(p1) or i2 < len(p2):
                    if i1 < len(p1):
                        order.append(p1[i1]); i1 += 1
                    if i2 < len(p2):
                        order.append(p2[i2]); i2 += 1
                for f_ in order:
                    f_()
                prev2 = s2_parts(*u) if u is not None else None
            cpr[0] = 2
            S.barrier()

        if last >= 5:
            areset()
            xT = take([128, KC, 512])
            cat = [take([128, KC, 512], BF16) for i in range(2)]
            G["sqr"] = take([128, 4, 512], BF16)
            G["rstd"] = take([128, 512])
            ffT = take([128, FC, 512], BF16)
            G["wb"] = [take([128, 11264], BF16) for i in range(2)]
            ld = []
            for q_ in range(5):
                if q_ < 4:
                    b_ = take([128, 512])
                    ld.append([b_, b_])
                else:
                    ld.append([take([128, 512]) for i in range(2)])
            ysum, ym, sq2, rs2 = take([128, 512]), take([128, 512]), take([128, 512]), take([128, 512])
            sil = take([128, 512])
            fsq = take([128, 2, 512], BF16)
            outF = ffT[:, 0:32, :].rearrange("p a b -> p (a b)").bitcast(F32).rearrange("p (k t) -> p k t", k=KC)
            bank_mode = [None]
            fb = [0]

            def nps_d():
                if bank_mode[0] == "front":
                    fb[0] ^= 1
                    return 6 + fb[0]
                psi[0] = (psi[0] + 1) % 6
                return psi[0]

            def front(t):
                t0 = t * 512
                ct, ck = cat[t % 2], f"cat{t % 2}"
                ba = nps_d()
                for j in range(8):
                    i = j % 2
                    dma(ld[4][i], attT_s[j * 128:(j + 1) * 128, t0:t0 + 512], ("attT_s",), (f"ld4{i}",))
                    act(fsq[:, i, :], ld[4][i], AF.Square, (f"ld4{i}",), (f"fsq{i}",))
                    mm(ps[ba][:, :], ones_b[:, :], fsq[:, i, :], j == 0, j == 7, (f"fsq{i}",) + C, (PK[ba],))
                act(rs2, ps[ba][:, :], AF.Sqrt, (PK[ba],) + C, ("rs2",), bias=epsc[:, 0:1], scale=1.0 / 1024)
                recip(rs2, rs2, ("rs2",), ("rs2",))
                yield
                for j in range(8):
                    i = j % 2
                    dma(ld[4][i], attT_s[j * 128:(j + 1) * 128, t0:t0 + 512], ("attT_s",), (f"ld4{i}",))
                    stt(ct[:, j, :], ld[4][i], c8[:, 0, j:j + 1], rs2, ALU.mult, ALU.mult, (f"ld4{i}", "rs2") + C, (ck,))
                    yield
                for j in range(8):
                    i = j % 2
                    jc = slice(j * 128, (j + 1) * 128)
                    dma(ld[0][i], y_s[0][jc, t0:t0 + 512], ("y_s0",), ("ld0",))
                    dma(ld[1][i], y_s[1][jc, t0:t0 + 512], ("y_s1",), ("ld1",))
                    dma(ld[2][i], bt_s[jc, t0:t0 + 512], ("bt_s",), ("ld2",))
                    dma(ld[3][i], g_s[jc, t0:t0 + 512], ("g_s",), ("ld3",))
                    tt("dve", ysum, ld[0][i], ld[1][i], ALU.add, ("ld0", "ld1"), ("ysum",))
                    b = nps_d()
                    mm(ps[b][:, :], blk_f[:, :], ysum, True, True, ("ysum",) + C, (PK[b],))
                    yield
                    stt(ym, ps[b][:, :], -1.0 / 64, ysum, ALU.mult, ALU.add, (PK[b], "ysum"), ("ym",))
                    act(sq2, ym, AF.Square, ("ym",), ("sq2",))
                    b = nps_d()
                    mm(ps[b][:, :], blk_f[:, :], sq2, True, True, ("sq2",) + C, (PK[b],))
                    yield
                    act(rs2, ps[b][:, :], AF.Sqrt, (PK[b],) + C, ("rs2",), bias=epsc[:, 1:2], scale=1.0 / 64)
                    recip(rs2, rs2, ("rs2",), ("rs2",))
                    yield
                    tt("dve", ym, ym, rs2, ALU.mult, ("ym", "rs2"), ("ym",))
                    ts("dve", ym, ym, c8[:, 4, j:j + 1], ALU.mult, ("ym",) + C, ("ym",), c8[:, 5, j:j + 1], ALU.add)
                    yield
                    tt("dve", ym, ym, ld[2][i], ALU.add, ("ym", "ld2"), ("ym",))
                    tt("dve", ct[:, 8 + j, :], ym, ld[3][i], ALU.mult, ("ym", "ld3"), (ck,))
                    yield

            def gemm_gen(wsrc, ncols, rhs_t, rhsk, nk, evac, gc=512):
                wb = G["wb"]
                c = 0
                while c * 128 < ncols:
                    gcols = min(gc, ncols - c * 128)
                    i = wbi[0] % 2
                    wbi[0] += 1
                    wv = wb[i][:, 0:nk * gcols].rearrange("p (k c) -> p k c", k=nk)
                    dma(wv, wsrc[:, c * 128:c * 128 + gcols].rearrange("(k p) c -> p k c", p=128),
                        (wsrc.name,), (f"wb{i}",))
                    for cc in range((gcols + 127) // 128):
                        b = nps_d()
                        for k in range(nk):
                            mm(ps[b][:, :], wv[:, k, cc * 128:cc * 128 + 128], rhs_t[:, k, :], k == 0, k == nk - 1,
                               (f"wb{i}", rhsk), (PK[b],))
                        evac(c + cc, 128, b)
                        yield
                    c += (gcols + 127) // 128

            def rms_d(src, srck, gi, dst_t, dstk):
                sqr, rstd = G["sqr"], G["rstd"]
                b = nps_d()
                for k in range(KC):
                    i = k % 4
                    act(sqr[:, i, :], src[:, k, :], AF.Square, (srck,), (f"sqr{i}",))
                    mm(ps[b][:, :], ones_b[:, :], sqr[:, i, :], k == 0, k == KC - 1, (f"sqr{i}",) + C, (PK[b],))
                act(rstd, ps[b][:, :], AF.Sqrt, (PK[b],) + C, ("rstd",), bias=epsc[:, 0:1], scale=1.0 / D)
                recip(rstd, rstd, ("rstd",), ("rstd",))
                for k in range(KC):
                    stt(dst_t[:, k, :], src[:, k, :], c16[:, gi, k:k + 1], rstd, ALU.mult, ALU.mult,
                        (srck, "rstd") + C, (dstk,))

            def main(t):
                t0 = t * 512
                ct, ck = cat[t % 2], f"cat{t % 2}"
                ostD = ct.rearrange("p a b -> p (a b)").bitcast(F32).rearrange("p (i d) -> p i d", i=2)
                dma(xT, xT_s[:, :, t0:t0 + 512].rearrange("k p t -> p k t"), ("xT_s",), ("xT",))

                def evac_x(c, m, b):
                    tt("dve", xT[:, c, :], xT[:, c, :], ps[b][:, :], ALU.add, ("xT", PK[b]), ("xT",))

                yield from gemm_gen(wbf_out, D, ct, ck, KC, evac_x)
                rms_d(xT, "xT", 1, ct, ck)
                yield
                for f in range(FC // 2):
                    i = wbi[0] % 2
                    wbi[0] += 1
                    wv = G["wb"][i][:, 0:KC * 512].rearrange("p (k c) -> p k c", k=KC)
                    dma(wv[:, :, 0:256], wbf_g[:, f * 256:(f + 1) * 256].rearrange("(k p) c -> p k c", p=128),
                        ("wbf_g",), (f"wb{i}",))
                    dma(wv[:, :, 256:512], wbf_u[:, f * 256:(f + 1) * 256].rearrange("(k p) c -> p k c", p=128),
                        ("wbf_u",), (f"wb{i}",))
                    for fc in range(2):
                        bg, bu = nps_d(), nps_d()
                        for k in range(KC):
                            mm(ps[bg][:, :], wv[:, k, fc * 128:(fc + 1) * 128], ct[:, k, :], k == 0, k == KC - 1,
                               (f"wb{i}", ck), (PK[bg],))
                        for k in range(KC):
                            mm(ps[bu][:, :], wv[:, k, 256 + fc * 128:256 + (fc + 1) * 128], ct[:, k, :], k == 0,
                               k == KC - 1, (f"wb{i}", ck), (PK[bu],))
                        act(sil, ps[bg][:, :], AF.Silu, (PK[bg],), ("sil",))
                        tt("dve", ffT[:, f * 2 + fc, :], sil, ps[bu][:, :], ALU.mult, ("sil", PK[bu]), ("ffT",))
                        yield
                yield from gemm_gen(wbf_d, D, ffT, "ffT", FC, evac_x, gc=256)
                rms_d(xT, "xT", 2, outF, "ffT")
                yield
                for b4 in range(4):
                    i = b4 % 2
                    for kq in range(4):
                        b = nps_d()
                        for k4 in range(4):
                            k = kq * 4 + k4
                            tr(ps[b][:, k4 * 128:(k4 + 1) * 128], outF[:, k, b4 * 128:(b4 + 1) * 128], ident_f[:],
                               ("ffT",) + C, (PK[b],))
                        cpa(ostD[:, i, kq * 512:(kq + 1) * 512], ps[b][:, :], (PK[b],), (ck,))
                    dst(y_out[t0 + b4 * 128:t0 + (b4 + 1) * 128, :], ostD[:, i, :], (ck,), ("y_out",))
                    yield

            def run_front(g, n):
                bank_mode[0] = "front"
                done = False
                for _ in range(n):
                    try:
                        next(g)
                    except StopIteration:
                        done = True
                        break
                bank_mode[0] = None
                return done

            run_front(front(0), 10 ** 9)
            for t in range(NT):
                fg = front(t + 1) if t + 1 < NT else None
                nm = 0
                for _ in main(t):
                    nm += 1
                    if fg is not None and nm > 16:
                        if run_front(fg, 2):
                            fg = None
                if fg is not None:
                    run_front(fg, 10 ** 9)
            S.barrier()

        blk = es.enter_context(nc.Block())

        def mk(e):
            def body(eng):
                S.replay(e, eng)
            return body

        blk.tensor(mk("pe"))
        blk.scalar(mk("act"))
        blk.vector(mk("dve"))
        blk.gpsimd(mk("pool"))
        blk.sync(mk("sp"))
    return nc


def _consts():
    p = np.arange(128)[:, None]
    f = np.arange(128)[None, :]
    ident = (p == f)
    msk = np.stack([np.stack([f > p, f >= p], 0), np.stack([f < p, f <= p], 0)], 0)
    msk = np.transpose(msk, (2, 0, 1, 3)).reshape(128, 512)
    mskl = np.stack([f < p, f > p], 0)
    mskl = np.transpose(mskl, (1, 0, 2)).reshape(128, 256)
    blk = ((p >= 64) == (f >= 64))
    rst = np.ones((128, 512), np.float32)
    rst[:, ::128] = 0.0
    return np.concatenate([ident, msk, mskl, blk, rst], axis=1).astype(np.float32)


def _pcol(v, k):
    return np.ascontiguousarray(np.asarray(v, np.float32).reshape(k, 128).T)


NEG = -30000.0


def _tables(rpb, RS):
    H = rpb.shape[0]
    c = np.arange(64)
    cs = np.clip(c - 8, 0, 48)
    cp_ = np.arange(64)[:, None]
    valid = (cp_ >= cs[None, :]) & (cp_ < cs[None, :] + 16)
    dcol = np.clip(cp_ - c[None, :] + 15, 0, 30)
    def tile_for(drows):
        nk = len(drows)
        out = np.full((nk, 64, H, 64), NEG, np.float32)
        for j, dr in enumerate(drows):
            if dr is None or dr < 0 or dr > 14:
                continue
            b = rpb[:, dr, :][:, dcol]
            b = np.where(valid[None], b, NEG)
            out[j] = np.transpose(b, (1, 0, 2))
        out = out.reshape(nk // 2, 2 * 64, H, 64)
        return np.transpose(out, (1, 2, 0, 3))
    types = [[j + 3 for j in range(8)]]
    for r in range(4):
        types.append([j - r + 7 for j in range(8)])
    for d in (2, 1, 0):
        types.append([j + d for j in range(8)])
    tab = np.stack([tile_for(t) for t in types], 0).astype(np.float32)
    return tab, tile_for


def _wide_tables(tile_for, RS, link):
    out = []
    R2 = 2 * RS
    for r in range(RS - 3, RS + 4):
        if link:
            r0 = int(np.clip(r - 4, 0, R2 - 8))
        else:
            base = 0 if r < RS else RS
            r0 = int(np.clip(r - 4, base, base + RS - 8))
        drows = []
        for jj in range(16):
            kr = RS - 8 + jj
            if r0 <= kr < r0 + 8:
                drows.append(kr - r + 7)
            else:
                drows.append(None)
        out.append(tile_for(drows))
    return np.stack(out, 0).astype(np.float32)


def make_in_map(xslot, link, P, RS):
    f = np.float32
    mu = np.zeros((2, 28 * 128), f)
    mu[0, :3488] = P["mu_prev"].reshape(-1)
    mu[1, :3488] = P["mu_next"].reshape(-1)
    mu = np.stack([_pcol(mu[0], 28), _pcol(mu[1], 28)], 1)
    pk16 = np.stack([_pcol(P["norm1_g"].reshape(-1), 16), _pcol(P["norm2_g"].reshape(-1), 16),
                     _pcol(P["final_g"].reshape(-1), 16)], 1)
    w0 = P["w0"].reshape(2, 1024)
    a0 = P["a0"].reshape(2, 1024)
    pk8 = np.stack([_pcol(P[k].reshape(-1), 8) for k in ("attn_out_g", "k_k", "k_a", "r_k", "lnx_w", "lnx_b")]
                   + [_pcol(w0[0], 8), _pcol(w0[1], 8), _pcol(a0[0], 8), _pcol(a0[1], 8)], 1)
    tab, tile_for = _tables(np.asarray(P["attn_rpb"], f)[0], RS)
    wtab = _wide_tables(tile_for, RS, link)
    return {
        "x": np.ascontiguousarray(xslot, dtype=f),
        "w_in": np.ascontiguousarray(P["w_in"][0], dtype=f), "w_out": np.ascontiguousarray(P["w_out"][0], dtype=f),
        "w_gate": np.ascontiguousarray(P["w_gate"][0], dtype=f), "w_up": np.ascontiguousarray(P["w_up"][0], dtype=f),
        "w_down": np.ascontiguousarray(P["w_down"][0], dtype=f),
        "pk16": np.ascontiguousarray(pk16), "pk8": np.ascontiguousarray(pk8), "mu": np.ascontiguousarray(mu),
        "w2": np.ascontiguousarray(np.asarray(P["w2"], f).reshape(128, 1024)),
        "a2": np.ascontiguousarray(np.asarray(P["a2"], f).reshape(128, 1024)),
        "g2": np.ascontiguousarray(np.asarray(P["g2"], f).reshape(160, 1024)),
        "link": np.full((128, 1), float(link), f), "cst": _consts(),
        "tab": np.ascontiguousarray(tab), "wtab": np.ascontiguousarray(wtab),
    }


_NC_CACHE = {}


def kernel(**inputs):
    RS = 128
    P = {k: np.asarray(v) for k, v in inputs.items() if not k.startswith("x_")}
    xp = np.asarray(inputs["x_prompt"], dtype=np.float32)
    xs = np.asarray(inputs["x_sample"], dtype=np.float32)
    T = 2 * RS * 64
    slots = [(xp[0:2].reshape(T, D), 0), (xp[2:4].reshape(T, D), 0), (xs[0].reshape(T, D), 1)]
    if RS not in _NC_CACHE:
        _NC_CACHE[RS] = build(RS)
    nc = _NC_CACHE[RS]
    maps = [make_in_map(x, link, P, RS) for x, link in slots]
    res = run_bass_kernel_spmd(nc, maps, core_ids=list(range(len(maps))))
    ys = [np.asarray(r["y"], dtype=np.float32) for r in res.results]
    y_prompt = np.concatenate([ys[0].reshape(2, T // 2, D), ys[1].reshape(2, T // 2, D)], axis=0)
    y_sample = ys[2].reshape(1, T, D)
    return (y_prompt, y_sample)
```
